# Optimizing a Trainium2 kernel written in Bass

```python
import math
import jax, jax.numpy as jnp
from jax import lax
import numpy as np

D_MODEL = 2048
BATCH = 4
SEQ = 4096
DEPTH = 4

CONV_CH = D_MODEL
CONV_WIDTH = 31
CONV_GROUPS = 16
SGU_CH = D_MODEL
SGU_GROUPS = 8
SGU_GROUP_CH = SGU_CH // SGU_GROUPS
CHUNK = 128
N_BRANCH = 2
FFN_HIDDEN = int(math.ceil(8 * D_MODEL / 3 / 256) * 256)
IN_WIDTH = 2 * CONV_CH + 2 * SGU_CH + N_BRANCH * D_MODEL
EPS = 1e-6

kernel_name = "hybrid_conformer_conv_gmlp_sandwich_block"


def rms_norm(x, g):
    xf = x.astype(jnp.float32)
    y = xf * lax.rsqrt(jnp.mean(xf * xf, axis=-1, keepdims=True) + EPS)
    return (y * g.astype(jnp.float32)).astype(x.dtype)


def layer_norm(x, g, b):
    xf = x.astype(jnp.float32)
    mu = jnp.mean(xf, axis=-1, keepdims=True)
    xc = xf - mu
    var = jnp.mean(xc * xc, axis=-1, keepdims=True)
    y = xc * lax.rsqrt(var + EPS)
    return (y * g.astype(jnp.float32) + b.astype(jnp.float32)).astype(x.dtype)


def conformer_conv_branch(a_in, a_gate, conv_w, conv_b, ln_g, ln_b, w_out):
    h = a_in * jax.nn.sigmoid(a_gate)
    rhs = conv_w[:, None, :].astype(h.dtype)
    h = lax.conv_general_dilated(
        h, rhs, window_strides=(1,), padding=[(CONV_WIDTH - 1, 0)],
        dimension_numbers=("NWC", "WIO", "NWC"),
        feature_group_count=CONV_CH)
    h = h + conv_b
    h = layer_norm(h, ln_g, ln_b)
    h = jax.nn.silu(h)
    return h @ w_out


def chunked_sgu_branch(b_in, ln_g, ln_b, w_spatial, b_spatial, w_out):
    bsz, seq, _ = b_in.shape
    z = jax.nn.gelu(b_in, approximate=False)
    u, v = jnp.split(z, 2, axis=-1)
    v = layer_norm(v, ln_g, ln_b)
    n_chunks = seq // CHUNK
    v = v.reshape(bsz, n_chunks, CHUNK, SGU_GROUPS, SGU_GROUP_CH)
    causal = jnp.tril(jnp.ones((CHUNK, CHUNK), dtype=bool))
    w_masked = jnp.where(causal[None], w_spatial, 0).astype(v.dtype)
    mixed = jnp.einsum("gts,bnsgc->bntgc", w_masked, v)
    mixed = mixed + jnp.transpose(b_spatial)[None, None, :, :, None].astype(v.dtype)
    mixed = mixed.reshape(bsz, seq, SGU_CH)
    return (u * mixed) @ w_out


def swiglu_ffn(h, w_gate_up, w_down):
    gu = h @ w_gate_up
    g, up = jnp.split(gu, 2, axis=-1)
    return (jax.nn.silu(g) * up) @ w_down


def setup_inputs(seed: int = 0) -> dict:
    key = jax.random.key(seed)
    ks = jax.random.split(key, 20)
    L = DEPTH
    f32 = jnp.float32

    def nrm(k, shape, scale):
        return jax.random.normal(k, shape, f32) * scale

    def gain(k, shape):
        return 1.0 + 0.05 * jax.random.normal(k, shape, f32)

    x = jax.random.normal(ks[0], (BATCH, SEQ, D_MODEL), f32)
    return {
        "x": x,
        "norm_mix_pre": gain(ks[1], (L, D_MODEL)),
        "norm_mix_post": gain(ks[2], (L, D_MODEL)),
        "norm_ffn_pre": gain(ks[3], (L, D_MODEL)),
        "norm_ffn_post": gain(ks[4], (L, D_MODEL)),
        "w_in": nrm(ks[5], (L, D_MODEL, IN_WIDTH), D_MODEL ** -0.5),
        "b_gate": nrm(ks[6], (L, N_BRANCH, D_MODEL), 0.01),
        "conv_w": nrm(ks[7], (L, CONV_WIDTH, CONV_CH), CONV_WIDTH ** -0.5),
        "conv_b": nrm(ks[8], (L, CONV_CH), 0.02),
        "conv_ln_g": gain(ks[9], (L, CONV_CH)),
        "conv_ln_b": nrm(ks[10], (L, CONV_CH), 0.02),
        "w_a_out": nrm(ks[11], (L, CONV_CH, D_MODEL), CONV_CH ** -0.5),
        "sgu_ln_g": gain(ks[12], (L, SGU_CH)),
        "sgu_ln_b": nrm(ks[13], (L, SGU_CH), 0.02),
        "w_spatial": nrm(ks[14], (L, SGU_GROUPS, CHUNK, CHUNK), CHUNK ** -0.5),
        "b_spatial": gain(ks[15], (L, SGU_GROUPS, CHUNK)),
        "w_b_out": nrm(ks[16], (L, SGU_CH, D_MODEL), SGU_CH ** -0.5),
        "w_o": nrm(ks[17], (L, D_MODEL, D_MODEL), D_MODEL ** -0.5),
        "w_gate_up": nrm(ks[18], (L, D_MODEL, 2 * FFN_HIDDEN), D_MODEL ** -0.5),
        "w_down": nrm(ks[19], (L, FFN_HIDDEN, D_MODEL), FFN_HIDDEN ** -0.5),
    }


def reference(x, norm_mix_pre, norm_mix_post, norm_ffn_pre, norm_ffn_post,
              w_in, b_gate, conv_w, conv_b, conv_ln_g, conv_ln_b, w_a_out,
              sgu_ln_g, sgu_ln_b, w_spatial, b_spatial, w_b_out, w_o,
              w_gate_up, w_down):
    bsz, seq, _ = x.shape
    split_pts = [CONV_CH, 2 * CONV_CH, 2 * CONV_CH + 2 * SGU_CH]
    for i in range(DEPTH):
        h = rms_norm(x, norm_mix_pre[i])
        proj = h @ w_in[i]
        a_in, a_gate, b_in, gate_logits = jnp.split(proj, split_pts, axis=-1)
        y_a = conformer_conv_branch(a_in, a_gate, conv_w[i], conv_b[i],
                                    conv_ln_g[i], conv_ln_b[i], w_a_out[i])
        y_b = chunked_sgu_branch(b_in, sgu_ln_g[i], sgu_ln_b[i], w_spatial[i],
                                 b_spatial[i], w_b_out[i])
        gates = jax.nn.sigmoid(gate_logits.reshape(bsz, seq, N_BRANCH, D_MODEL) + b_gate[i])
        merged = gates[:, :, 0, :] * y_a + gates[:, :, 1, :] * y_b
        x = x + rms_norm(merged @ w_o[i], norm_mix_post[i])
        h = rms_norm(x, norm_ffn_pre[i])
        x = x + rms_norm(swiglu_ffn(h, w_gate_up[i], w_down[i]), norm_ffn_post[i])
    return x
```

```python
from contextlib import ExitStack

import numpy as np
import concourse.bass as bass
import concourse.mybir as mybir
from concourse.bass_utils import run_bass_kernel_spmd

F32 = mybir.dt.float32
BF16 = mybir.dt.bfloat16
AF = mybir.ActivationFunctionType
ALU = mybir.AluOpType

D = 2048
BATCH = 4
SEQ = 4096
DEPTH = 4
FF = 5632
NJ = FF // 128
CW = 31
EPS = 1e-6
NCORES = 8
HALO = 256
TOK = SEQ // 2 + HALO
GROUPS = [512, 512, 512, 512, HALO]
NMAX = 512
NV = 9
(V_MIX_PRE, V_MIX_POST, V_FFN_PRE, V_FFN_POST, V_CONV_B, V_CLN_G, V_CLN_B, V_BG0, V_BG1) = range(NV)
NSLOT = 4
WBLK = 16 * 256


class View:
    __slots__ = ("ap", "reg")

    def __init__(self, ap, reg):
        self.ap = ap
        self.reg = reg


class Buf:
    def __init__(self, base, ap2d, off, esize, ncols, tracked=True):
        self.base = base
        self.a = ap2d
        self.off = off
        self.es = esize
        self.ncols = ncols
        self.tracked = tracked

    def v(self, lo=0, hi=None, p0=None, p1=None):
        hi = self.ncols if hi is None else hi
        ap = self.a[:, lo:hi] if p0 is None else self.a[p0:p1, lo:hi]
        reg = (self.base, self.off + lo * self.es, self.off + hi * self.es) if self.tracked else None
        return View(ap, reg)


class Sched:
    ENG = ("pe", "act", "dve", "pool", "sp")

    def __init__(self, nc, stack):
        self.nc = nc
        self.stack = stack
        self.e = {"pe": nc.tensor, "act": nc.scalar, "dve": nc.vector, "pool": nc.gpsimd, "sp": nc.sync}
        self.semh = {}
        self.cnt = {}
        for n in ("pe", "act", "dve", "pool"):
            self.new_sem(n)
        self.waited = {n: {} for n in self.ENG}
        self.recs = {}
        self.ninst = 0
        self.nwaits = 0
        self.last_pe = None
        self.phase = "setup"
        self.pe_labels = []

    def new_sem(self, name):
        self.semh[name] = self.stack.enter_context(self.nc.semaphore(name))
        self.cnt[name] = 0
        return name

    def _collect(self, reads, writes):
        need = {}
        for r in reads:
            if r is None:
                continue
            base, lo, hi = r
            for rec in self.recs.get(base, ()):
                if rec[2] and rec[0] < hi and lo < rec[1]:
                    s, v = rec[3]
                    if need.get(s, 0) < v:
                        need[s] = v
        for w in writes:
            if w is None:
                continue
            base, lo, hi = w
            for rec in self.recs.get(base, ()):
                if rec[0] < hi and lo < rec[1]:
                    s, v = rec[3]
                    if need.get(s, 0) < v:
                        need[s] = v
        return need

    def _record(self, reads, writes, tok):
        for w in writes:
            if w is None:
                continue
            base, lo, hi = w
            lst = self.recs.setdefault(base, [])
            lst[:] = [r for r in lst if not (lo <= r[0] and r[1] <= hi)]
            lst.append([lo, hi, True, tok])
        for r in reads:
            if r is None:
                continue
            base, lo, hi = r
            lst = self.recs.setdefault(base, [])
            for rec in lst:
                if (not rec[2]) and rec[0] == lo and rec[1] == hi and rec[3][0] == tok[0]:
                    if rec[3][1] < tok[1]:
                        rec[3] = tok
                    break
            else:
                lst.append([lo, hi, False, tok])

    def _emit_waits(self, eng, need):
        w = self.waited[eng]
        for s, v in need.items():
            if eng == "pe" and s == "pe":
                continue
            if s == "pe" and v > self.cnt["pe"]:
                assert v == self.cnt["pe"] + 1 and self.last_pe is not None and not self.last_pe[1]
                self.last_pe[0].then_inc(self.semh["pe"], 1)
                self.last_pe[1] = True
                self.cnt["pe"] += 1
            if w.get(s, 0) < v:
                self.e[eng].wait_ge(self.semh[s], v)
                w[s] = v
                self.nwaits += 1

    def op(self, eng, fn, reads=(), writes=(), inc=True):
        reads = [r.reg for r in reads]
        writes = [w.reg for w in writes]
        self._emit_waits(eng, self._collect(reads, writes))
        ins = fn(self.e[eng])
        self.ninst += 1
        if eng == "pe":
            self.last_pe = [ins, bool(inc)]
            self.pe_labels.append(self.phase)
        if inc:
            self.cnt[eng] += 1
            ins.then_inc(self.semh[eng], 1)
            tok = (eng, self.cnt[eng])
        else:
            tok = (eng, self.cnt[eng] + 1)
        self._record(reads, writes, tok)
        return tok

    def dma(self, eng, sem, out, in_, reads=(), writes=(), **kw):
        reads = [r if (r is None or isinstance(r, tuple)) else r.reg for r in reads]
        writes = [w if (w is None or isinstance(w, tuple)) else w.reg for w in writes]
        self._emit_waits(eng, self._collect(reads, writes))
        ins = self.e[eng].dma_start(out=out, in_=in_, **kw)
        self.ninst += 1
        self.cnt[sem] += 16
        ins.then_inc(self.semh[sem], 16)
        tok = (sem, self.cnt[sem])
        self._record(reads, writes, tok)
        return tok

    def wait_all(self, eng):
        self._emit_waits(eng, {s: v for s, v in self.cnt.items() if v > 0})


def build(L, groups=GROUPS, stage=9):
    nc = bass.Bass("TRN2", target_bir_lowering=False)
    ntok = sum(groups)
    dr = {}

    def din(name, shape):
        dr[name] = nc.dram_tensor(name, shape, F32, kind="ExternalInput").ap()
        return dr[name]

    x_d = din("x", [ntok, D])
    hm_d = din("hm", [128, 1])
    pv_d = din("pv", [128, L * NV * 16])
    cw_d = din("cw", [L, 128, 16 * CW])
    gb_d = din("gb", [L, 128, 2 * D])
    wst_d = din("wst", [128, L * 8 * 128])
    bsp_d = din("bsp", [1, L * 8 * 128])
    w_in = din("w_in", [L, D, 6 * D])
    w_a = din("w_a", [L, D, D])
    w_b = din("w_b", [L, D, D])
    w_o = din("w_o", [L, D, D])
    w_gu = din("w_gu", [L, D, 2 * FF])
    w_d = din("w_d", [L, FF, D])
    y_d = nc.dram_tensor("y", [ntok, D], F32, kind="ExternalOutput").ap()

    MATS = {"win": (w_in, 48), "wa": (w_a, 8), "wb": (w_b, 8), "wo": (w_o, 8), "wgu": (w_gu, 44), "wd": (w_d, 24)}
    sc = {}
    for l in range(L):
        for mname, (_, nb_) in MATS.items():
            sc[(mname, l)] = nc.dram_tensor(f"sc_{mname}_{l}", [nb_, 128, WBLK], BF16, kind="Internal").ap()

    with ExitStack() as st:
        S = Sched(nc, st)

        plan = []
        off = [0]
        bufs = {}

        def alloc(name, dt, es, ncols, tracked=True):
            o = off[0]
            plan.append((name, dt, es, ncols, tracked, o))
            off[0] = o + ((es * ncols + 63) // 64) * 64
            return o

        alloc("x", F32, 4, 16 * NMAX)
        o_hA = alloc("h", BF16, 2, 16 * NMAX)
        alloc("A", BF16, 2, 16 * NMAX)
        o_uvu = alloc("UB", BF16, 2, 16 * NMAX)
        alloc("VM", BF16, 2, 16 * NMAX)
        alloc("U", BF16, 2, 16 * NMAX)
        for i in range(NSLOT):
            alloc(f"W{i}", BF16, 2, WBLK)
        alloc("GB", F32, 4, 2 * D)
        for i in range(6):
            alloc(f"T{i}", F32, 4, NMAX)
        for i in range(3):
            alloc(f"hat{i}", BF16, 2, NMAX + 32)
        for i in range(12):
            alloc(f"dg{i}", BF16, 2, 128)
        alloc("identb", BF16, 2, 128)
        for i in range(4):
            alloc(f"sq{i}", BF16, 2, NMAX)
        alloc("pv", F32, 4, L * NV * 16, False)
        alloc("cw", F32, 4, 16 * CW)
        alloc("state", BF16, 2, L * 16 * 30)
        alloc("wsT", BF16, 2, L * 8 * 128)
        alloc("bspT", BF16, 2, L * 8 * 128)
        alloc("ones", BF16, 2, 128)
        alloc("ident", F32, 4, 128)
        alloc("hm", F32, 4, 1, False)
        alloc("vs1", F32, 4, 32)
        alloc("vs2", F32, 4, 32)
        alloc("vst", F32, 4, 32)
        total = off[0]
        arena = st.enter_context(nc.sbuf_tensor("arena", [128, total // 2], BF16))
        a_bf = arena[:, :]
        a_f32 = a_bf.bitcast(F32)
        for (name, dt, es, ncols, tracked, o) in plan:
            base_ap = a_f32 if dt == F32 else a_bf
            bufs[name] = Buf("arena", base_ap[:, o // es: o // es + ncols], o, es, ncols, tracked)
        bufs["Y"] = Buf("arena", a_f32[:, o_hA // 4: o_hA // 4 + 16 * NMAX], o_hA, 4, 16 * NMAX)
        bufs["C"] = Buf("arena", a_bf[:, o_uvu // 2: o_uvu // 2 + 16 * NMAX], o_uvu, 2, 16 * NMAX)
        bufs["STG"] = Buf("arena", a_f32[:, o_uvu // 4: o_uvu // 4 + 24 * NMAX], o_uvu, 4, 24 * NMAX)
        bufs["F"] = Buf("arena", a_bf[:, o_uvu // 2: o_uvu // 2 + NJ * NMAX], o_uvu, 2, NJ * NMAX)
        B = bufs
        X, H, A_, UB, VM, U, GB, Y, C, Fh = B["x"], B["h"], B["A"], B["UB"], B["VM"], B["U"], B["GB"], B["Y"], B["C"], B["F"]
        T = [B[f"T{i}"] for i in range(6)]
        HAT = [B[f"hat{i}"] for i in range(3)]
        SQ = [B[f"sq{i}"] for i in range(4)]
        PV, CWB, STATE, WST, BSPT, ONES, IDENT, HM = B["pv"], B["cw"], B["state"], B["wsT"], B["bspT"], B["ones"], B["ident"], B["hm"]
        VS1, VS2, VST = B["vs1"], B["vs2"], B["vst"]
        DG = [B[f"dg{i}"] for i in range(12)]
        IDENTB = B["identb"]
        rot_dg = [0]

        banks = []
        for i in range(8):
            h_ = st.enter_context(nc.psum_tensor(f"pb{i}", [128, 512], F32))
            banks.append(Buf(f"pb{i}", h_[:, :], 0, 4, 512))
        rot = {"b": 0, "t": 0, "q": 0, "h": 0}

        def nb():
            b = banks[rot["b"] % 4]
            rot["b"] += 1
            return b

        def nsq():
            b = SQ[rot["q"] % 4]
            rot["q"] += 1
            return b

        SB1, SB2 = banks[6], banks[7]
        CB = [banks[4], banks[5]]

        for s_ in ("cst", "gbs", "cws") + tuple(f"xl{i}" for i in range(4)) + tuple(f"ys{i}" for i in range(4)):
            S.new_sem(s_)
        for i in range(NSLOT):
            S.new_sem(f"ws{i}")
        for i in range(NSLOT):
            S.new_sem(f"wb{i}")
            S.new_sem(f"wp{i}")

        def act(out, in_, func, bias=None, scale=None, accum=None, extra=()):
            kw = {}
            if bias is not None:
                kw["bias"] = bias
            if scale is not None:
                kw["scale"] = scale
            if accum is not None:
                kw["accum_out"] = accum.ap
            w = [out] + ([accum] if accum is not None else [])
            return S.op("act", lambda e: e.activation(out.ap, in_.ap, func, **kw), reads=[in_] + list(extra), writes=w)

        def tt(eng, out, a, b, op):
            return S.op(eng, lambda e: e.tensor_tensor(out.ap, a.ap, b.ap, op), reads=[a, b], writes=[out])

        def stt(out, in0, scalar, in1, op0, op1, extra=()):
            return S.op("dve", lambda e: e.scalar_tensor_tensor(out.ap, in0.ap, scalar, in1.ap, op0, op1),
                        reads=[in0, in1] + list(extra), writes=[out])

        def ts(eng, out, in0, s1, s2, op0, op1=None, extra=()):
            if op1 is None:
                return S.op(eng, lambda e: e.tensor_scalar(out.ap, in0.ap, s1, None, op0), reads=[in0] + list(extra), writes=[out])
            return S.op(eng, lambda e: e.tensor_scalar(out.ap, in0.ap, s1, s2, op0, op1), reads=[in0] + list(extra), writes=[out])

        def rsqrt_(v):
            S.op("dve", lambda e: e.reciprocal(v.ap, v.ap), reads=[v], writes=[v])
            act(v, v, AF.Sqrt)

        def cp(eng, out, in_):
            if eng == "act":
                return S.op("act", lambda e: e.copy(out.ap, in_.ap), reads=[in_], writes=[out])
            return S.op(eng, lambda e: e.tensor_copy(out.ap, in_.ap), reads=[in_], writes=[out])

        def mm(out, lhsT, rhs, start, stop, inc=None):
            return S.op("pe", lambda e: e.matmul(out.ap, lhsT.ap, rhs.ap, start=start, stop=stop),
                        reads=[lhsT, rhs], writes=[out], inc=(stop if inc is None else inc))

        def tr(out, in_):
            return S.op("pe", lambda e: e.transpose(out.ap, in_.ap, IDENT.v().ap), reads=[in_], writes=[out])

        deferred = []

        def flush():
            while deferred:
                deferred.pop(0)()

        def pvs(l, vec, c):
            col = (l * NV + vec) * 16 + c
            return PV.v(col, col + 1).ap

        def convert_layer(l):
            for mname, (wsrc, nblk) in MATS.items():
                sem = f"cv_{mname}_{l}"
                for b in range(nblk):
                    if mname == "wd":
                        cb, kr = divmod(b, 3)
                        kk = 16 if kr < 2 else 12
                        src = wsrc[l, kr * 2048: kr * 2048 + kk * 128, cb * 256:(cb + 1) * 256]
                    else:
                        kk = 16
                        src = wsrc[l, :, b * 256:(b + 1) * 256]
                    src = src.rearrange("(k p) c -> p k c", p=128)
                    dst = sc[(mname, l)][b][:, 0:kk * 256].rearrange("p (k c) -> p k c", k=kk)
                    S.dma("pool", sem, dst, src, writes=[(f"sc_{mname}_{l}", 0, 1)])

        def layer_blocks(l):
            seq = []
            for j in range(8):
                seq += [("win", l, 8 + j, 16), ("win", l, j, 16), ("win", l, 24 + j, 16)]
            for j in range(8):
                seq.append(("win", l, 16 + j, 16))
            for j in range(8):
                seq += [("win", l, 32 + j, 16), ("win", l, 40 + j, 16), ("wa", l, j, 16), ("wb", l, j, 16)]
            for j in range(8):
                seq.append(("wo", l, j, 16))
            for j in range(22):
                seq += [("wgu", l, j, 16), ("wgu", l, 22 + j, 16)]
            for cb in range(8):
                for kr in range(3):
                    seq.append(("wd", l, cb * 3 + kr, 16 if kr < 2 else 12))
            return seq

        wseq = []
        for _g in groups:
            for l in range(L):
                lb = layer_blocks(l)
                if stage == 1:
                    lb = [b_ for b_ in lb if b_[0] not in ("wgu", "wd")]
                if stage >= 1:
                    wseq += lb
        wst_ = {"issued": 0, "next": 0}
        WS = [B[f"W{i}"] for i in range(NSLOT)]

        first_pass = len(layer_blocks(0)) * L

        def w_src(mname, l, b, kk):
            wsrc = MATS[mname][0]
            if mname == "wd":
                cb, kr = divmod(b, 3)
                src = wsrc[l, kr * 2048: kr * 2048 + kk * 128, cb * 256:(cb + 1) * 256]
            else:
                src = wsrc[l, :, b * 256:(b + 1) * 256]
            return src.rearrange("(k p) c -> p k c", p=128)

        def w_issue():
            i = wst_["issued"]
            if i >= len(wseq):
                return
            mname, l, b, kk = wseq[i]
            sl = i % NSLOT
            slot = WS[sl].v(0, kk * 256)
            blk = (f"sc_{mname}_{l}", b, b + 1)
            if i < first_pass:
                S.dma("pool", f"wp{sl}", slot.ap.rearrange("p (k c) -> p k c", k=kk), w_src(mname, l, b, kk), writes=[slot])
                S.dma("sp", f"wb{sl}", sc[(mname, l)][b][:, 0:kk * 256], slot.ap, reads=[slot], writes=[blk])
            else:
                S.dma("sp", f"ws{sl}", slot.ap, sc[(mname, l)][b][:, 0:kk * 256], reads=[blk], writes=[slot])
            wst_["issued"] = i + 1

        def wget(desc):
            i = wst_["next"]
            assert wseq[i][:3] == desc, (wseq[i], desc)
            while wst_["issued"] < min(i + NSLOT, len(wseq)):
                w_issue()
            wst_["next"] = i + 1
            return WS[i % NSLOT]

        stg = B["STG"]
        LW = L * 1024
        o_b0, o_b1 = LW, 2 * LW
        S.dma("sp", "cst", PV.v().ap, pv_d, writes=[])
        S.dma("sp", "cst", HM.v().ap, hm_d, writes=[])
        S.dma("sp", "cst", stg.v(0, LW).ap, wst_d, writes=[stg.v(0, LW)])
        S.dma("sp", "cst", stg.v(o_b0, o_b0 + LW, 0, 1).ap, bsp_d, writes=[stg.v(o_b0, o_b0 + LW)])
        cst_all = {"cst": S.cnt["cst"]}
        S._emit_waits("pool", cst_all)
        S._emit_waits("dve", cst_all)
        S.op("pool", lambda e: e.memset(ONES.v().ap, 1.0), writes=[ONES.v()])
        S.op("pool", lambda e: e.memset(IDENT.v().ap, 1.0), writes=[IDENT.v()])
        S.op("pool", lambda e: e.memset(STATE.v().ap, 0.0), writes=[STATE.v()])
        S.op("pool", lambda e: e.affine_select(out=IDENT.v().ap, in_=IDENT.v().ap, pattern=[[-1, 128]],
                                               compare_op=ALU.is_equal, fill=0.0, base=0, channel_multiplier=1),
             reads=[IDENT.v()], writes=[IDENT.v()])
        S.op("pool", lambda e: e.tensor_copy(IDENTB.v().ap, IDENT.v().ap), reads=[IDENT.v()], writes=[IDENTB.v()])
        S.op("pool", lambda e: e.affine_select(out=stg.v(0, LW).ap, in_=stg.v(0, LW).ap,
                                               pattern=[[0, L * 8], [1, 128]], compare_op=ALU.is_ge, fill=0.0,
                                               base=0, channel_multiplier=-1),
             reads=[stg.v(0, LW)], writes=[stg.v(0, LW)])
        S.op("pool", lambda e: e.tensor_copy(WST.v().ap, stg.v(0, LW).ap), reads=[stg.v(0, LW)], writes=[WST.v()])
        bs0 = stg.v(o_b0, o_b0 + LW)
        bs1 = stg.v(o_b1, o_b1 + LW)
        r0 = lambda bf, o: bf.v(o, o + LW, 0, 1).ap
        lo_bf = Fh.v(2 * LW, 3 * LW)
        S.op("dve", lambda e: e.tensor_copy(r0(BSPT, 0), r0(stg, o_b0)), reads=[bs0], writes=[BSPT.v()])
        S.op("dve", lambda e: e.tensor_copy(r0(stg, o_b1), r0(BSPT, 0)), reads=[BSPT.v()], writes=[bs1])
        S.op("dve", lambda e: e.tensor_tensor(r0(stg, o_b1), r0(stg, o_b0), r0(stg, o_b1), ALU.subtract),
             reads=[bs0, bs1], writes=[bs1])
        S.op("dve", lambda e: e.tensor_copy(r0(Fh, 2 * LW), r0(stg, o_b1)), reads=[bs1, bs0], writes=[lo_bf])
        S.dma("sp", "cst", BSPT.v(0, LW, 1, 2).ap, r0(Fh, 2 * LW), reads=[lo_bf], writes=[BSPT.v()])
        setup_need = {s_: v for s_, v in S.cnt.items() if v > 0 and not s_.startswith("cv_")}
        for eng in ("pe", "act", "dve", "pool"):
            S._emit_waits(eng, dict(setup_need))

        def chunk(buf, c, n, width=NMAX):
            return buf.v(c * width, c * width + n)

        def load_layer_consts(l):
            S.dma("sp", "gbs", GB.v().ap, gb_d[l], writes=[GB.v()])
            S.dma("sp", "cws", CWB.v().ap, cw_d[l], writes=[CWB.v()])

        def rms_to_h(l, vec, n):
            S.phase = "rms_pre"
            for c in range(16):
                q = nsq().v(0, n)
                act(q, chunk(X, c, n), AF.Square)
                mm(SB1.v(0, n), ONES.v(), q, c == 0, c == 15)
            r = T[4].v(0, n)
            ts("dve", r, SB1.v(0, n), 1.0 / D, EPS, ALU.mult, ALU.add)
            rsqrt_(r)
            for c in range(16):
                stt(chunk(H, c, n), chunk(X, c, n), pvs(l, vec, c), r, ALU.mult, ALU.mult)

        def post_norm_residual(l, vec, n):
            r = T[4].v(0, n)
            ts("dve", r, SB1.v(0, n), 1.0 / D, EPS, ALU.mult, ALU.add)
            rsqrt_(r)
            for m in range(16):
                ym = chunk(Y, m, n)
                stt(ym, ym, pvs(l, vec, m), r, ALU.mult, ALU.mult)
                tt("dve", chunk(X, m, n), chunk(X, m, n), ym, ALU.add)

        def proj_fm(wslot, mm_, src, n, nk=16, k0=0, pb=None, start=True, stop=True):
            if pb is None:
                pb = nb()
            for k in range(nk):
                mm(pb.v(0, n), wslot.v(k * 256 + mm_ * 128, k * 256 + (mm_ + 1) * 128), chunk(src, k0 + k, n),
                   start and k == 0, stop and k == nk - 1)
            return pb

        def mixer(l, n, gi):
            NT = n // 128
            load_layer_consts(l)
            rms_to_h(l, V_MIX_PRE, n)
            for j in range(8):
                S.phase = "a_gate"
                wg = wget(("win", l, 8 + j))
                sg = []
                for mm_ in range(2):
                    pb = proj_fm(wg, mm_, H, n)
                    t_ = T[(2 * j + mm_) % 4].v(0, n)
                    act(t_, pb.v(0, n), AF.Sigmoid)
                    sg.append(t_)
                S.phase = "a_in"
                wa = wget(("win", l, j))
                pbs = [proj_fm(wa, mm_, H, n) for mm_ in range(2)]
                S.phase = "a_fin_stats"
                flush()
                hats = []
                for mm_ in range(2):
                    m = 2 * j + mm_
                    hat = HAT[rot["h"] % 3]
                    rot["h"] += 1
                    sto = (l * 16 + m) * 30
                    cp("dve", hat.v(0, 30), STATE.v(sto, sto + 30))
                    tt("dve", hat.v(30, 30 + n), pbs[mm_].v(0, n), sg[mm_], ALU.mult)
                    if gi == 0:
                        ts("dve", hat.v(30, 30 + HALO), hat.v(30, 30 + HALO), HM.v().ap, None, ALU.mult)
                    cp("dve", STATE.v(sto, sto + 30), hat.v(n, n + 30))
                    hats.append(hat)
                S.phase = "conv"
                for mm_ in range(2):
                    m = 2 * j + mm_
                    acc = CB[m % 2].v(0, n)
                    cwc = m * CW
                    for k in range(CW):
                        dg = DG[rot_dg[0] % 12].v()
                        rot_dg[0] += 1
                        S.op("dve", lambda e: e.tensor_scalar(dg.ap, IDENTB.v().ap, CWB.v(cwc + k, cwc + k + 1).ap, None, ALU.mult),
                             reads=[CWB.v(cwc + k, cwc + k + 1)], writes=[dg])
                        mm(acc, dg, hats[mm_].v(k, k + n), k == 0, k == CW - 1)

                    def fin(m=m, acc=acc):
                        cm = chunk(C, m, n)
                        act(cm, acc, AF.Identity, bias=pvs(l, V_CONV_B, m))
                        q2 = nsq().v(0, n)
                        act(q2, acc, AF.Square, bias=pvs(l, V_CONV_B, m))
                        mm(SB1.v(0, n), ONES.v(), cm, m == 0, m == 15)
                        mm(SB2.v(0, n), ONES.v(), q2, m == 0, m == 15)
                    deferred.append(fin)
                S.phase = "v_proj"
                wv = wget(("win", l, 24 + j))
                for t in range(NT):
                    pb = nb()
                    for k in range(16):
                        mm(pb.v(0, 256), H.v(k * NMAX + t * 128, k * NMAX + (t + 1) * 128), wv.v(k * 256, (k + 1) * 256),
                           k == 0, k == 15)
                    vb = VM.v(t * D + j * 256, t * D + (j + 1) * 256)
                    act(vb, pb.v(0, 256), AF.Gelu, accum=VS1.v(t * 8 + j, t * 8 + j + 1))
                    act(nsq().v(0, 256), vb, AF.Square, accum=VS2.v(t * 8 + j, t * 8 + j + 1))
            S.phase = "a_fin_last"
            flush()
            mean = T[4].v(0, n)
            rstd = T[5].v(0, n)
            tmp = T[0].v(0, n)
            ts("dve", mean, SB1.v(0, n), 1.0 / D, None, ALU.mult)
            tt("dve", tmp, mean, mean, ALU.mult)
            stt(rstd, SB2.v(0, n), 1.0 / D, tmp, ALU.mult, ALU.subtract)
            ts("dve", rstd, rstd, EPS, None, ALU.add)
            rsqrt_(rstd)
            stt(mean, mean, -1.0, rstd, ALU.mult, ALU.mult)
            s1, s2 = VST.v(12, 12 + NT), VST.v(16, 16 + NT)
            mean_t, msq_t, rstd_t = VST.v(0, NT), VST.v(4, 4 + NT), VST.v(8, 8 + NT)
            S.op("dve", lambda e: e.tensor_reduce(s1.ap, VS1.v(0, NT * 8).ap.rearrange("p (t j) -> p t j", j=8),
                                                  mybir.AxisListType.X, ALU.add), reads=[VS1.v(0, NT * 8)], writes=[s1])
            S.op("dve", lambda e: e.tensor_reduce(s2.ap, VS2.v(0, NT * 8).ap.rearrange("p (t j) -> p t j", j=8),
                                                  mybir.AxisListType.X, ALU.add), reads=[VS2.v(0, NT * 8)], writes=[s2])
            ts("dve", mean_t, s1, 1.0 / D, None, ALU.mult)
            tt("dve", msq_t, mean_t, mean_t, ALU.mult)
            stt(rstd_t, s2, 1.0 / D, msq_t, ALU.mult, ALU.subtract)
            ts("dve", rstd_t, rstd_t, EPS, None, ALU.add)
            rsqrt_(rstd_t)
            nmr_t = VST.v(20, 20 + NT)
            stt(nmr_t, mean_t, -1.0, rstd_t, ALU.mult, ALU.mult)
            tasks = []
            rot_t = [0]

            def v_task(t, q):
                vb = VM.v(t * D + q * 512, t * D + (q + 1) * 512)
                tmp = T[rot_t[0] % 4].v(0, 512)
                rot_t[0] += 1
                act(tmp, vb, AF.Identity, bias=VST.v(20 + t, 21 + t).ap, scale=VST.v(8 + t, 9 + t).ap,
                    extra=[nmr_t, rstd_t])
                tt("dve", tmp, tmp, GB.v(q * 512, (q + 1) * 512), ALU.mult)
                tt("dve", vb, tmp, GB.v(D + q * 512, D + (q + 1) * 512), ALU.add)

            def a_task(m):
                t_ = T[rot_t[0] % 4].v(0, n)
                rot_t[0] += 1
                tt("dve", t_, chunk(C, m, n), rstd, ALU.mult)
                tt("dve", t_, t_, mean, ALU.add)
                act(chunk(A_, m, n), t_, AF.Silu, bias=pvs(l, V_CLN_B, m), scale=pvs(l, V_CLN_G, m))

            for t in range(NT):
                for q in range(4):
                    tasks.append(lambda t=t, q=q: v_task(t, q))
            for m in range(16):
                tasks.append(lambda m=m: a_task(m))
            S.phase = "u_proj"
            for j in range(8):
                wu = wget(("win", l, 16 + j))
                for mm_ in range(2):
                    pb = proj_fm(wu, mm_, H, n)
                    act(chunk(U, 2 * j + mm_, n), pb.v(0, n), AF.Gelu)
                    for _ in range(2):
                        if tasks:
                            tasks.pop(0)()
            while tasks:
                tasks.pop(0)()
            S.phase = "spatial"
            for m in range(16):
                g8 = l * 8 + m // 2
                pb = nb()
                for t in range(NT):
                    o = pb.v(t * 128, (t + 1) * 128)
                    mm(o, VM.v(t * D + m * 128, t * D + (m + 1) * 128), WST.v(g8 * 128, (g8 + 1) * 128), True, False, inc=False)
                    mm(o, ONES.v(0, 128, 0, 2), BSPT.v(g8 * 128, (g8 + 1) * 128, 0, 2), False, True, inc=(t == NT - 1))
                tt("dve", chunk(UB, m, n), pb.v(0, n), chunk(U, m, n), ALU.mult)
            S.phase = "merge"
            for j in range(8):
                g0 = []
                w0 = wget(("win", l, 32 + j))
                for mm_ in range(2):
                    pb = proj_fm(w0, mm_, H, n)
                    t_ = T[mm_].v(0, n)
                    act(t_, pb.v(0, n), AF.Sigmoid, bias=pvs(l, V_BG0, 2 * j + mm_))
                    g0.append(t_)
                g1 = []
                w1 = wget(("win", l, 40 + j))
                for mm_ in range(2):
                    pb = proj_fm(w1, mm_, H, n)
                    t_ = T[2 + mm_].v(0, n)
                    act(t_, pb.v(0, n), AF.Sigmoid, bias=pvs(l, V_BG1, 2 * j + mm_))
                    g1.append(t_)
                wa_ = wget(("wa", l, j))
                for mm_ in range(2):
                    pb = proj_fm(wa_, mm_, A_, n)
                    tt("dve", g0[mm_], pb.v(0, n), g0[mm_], ALU.mult)
                wb_ = wget(("wb", l, j))
                for mm_ in range(2):
                    pb = proj_fm(wb_, mm_, UB, n)
                    tt("dve", g1[mm_], pb.v(0, n), g1[mm_], ALU.mult)
                    tt("dve", chunk(VM, 2 * j + mm_, n), g0[mm_], g1[mm_], ALU.add)
            S.phase = "wo"
            for j in range(8):
                wo_ = wget(("wo", l, j))
                for mm_ in range(2):
                    m = 2 * j + mm_
                    pb = proj_fm(wo_, mm_, VM, n)
                    flush()
                    cp("act", chunk(Y, m, n), pb.v(0, n))
                    q = nsq().v(0, n)
                    act(q, pb.v(0, n), AF.Square)
                    deferred.append(lambda m=m, q=q: mm(SB1.v(0, n), ONES.v(), q, m == 0, m == 15))
            flush()
            post_norm_residual(l, V_MIX_POST, n)

        def ffn(l, n):
            rms_to_h(l, V_FFN_PRE, n)
            S.phase = "gate_up"
            for j in range(22):
                wg = wget(("wgu", l, j))
                sl = []
                for mm_ in range(2):
                    pb = proj_fm(wg, mm_, H, n)
                    t_ = T[(2 * j + mm_) % 4].v(0, n)
                    act(t_, pb.v(0, n), AF.Silu)
                    sl.append(t_)
                wu = wget(("wgu", l, 22 + j))
                for mm_ in range(2):
                    pb = proj_fm(wu, mm_, H, n)
                    tt("dve", chunk(Fh, 2 * j + mm_, n), pb.v(0, n), sl[mm_], ALU.mult)
            S.phase = "down"
            for cb in range(8):
                pbs = [nb(), nb()]
                for kr in range(3):
                    kk = 16 if kr < 2 else 12
                    wd_ = wget(("wd", l, cb * 3 + kr))
                    for mm_ in range(2):
                        proj_fm(wd_, mm_, Fh, n, nk=kk, k0=kr * 16, pb=pbs[mm_], start=(kr == 0), stop=(kr == 2))
                flush()
                for mm_ in range(2):
                    m = 2 * cb + mm_
                    cp("act", chunk(Y, m, n), pbs[mm_].v(0, n))
                    q = nsq().v(0, n)
                    act(q, pbs[mm_].v(0, n), AF.Square)
                    deferred.append(lambda m=m, q=q: mm(SB1.v(0, n), ONES.v(), q, m == 0, m == 15))
            flush()
            post_norm_residual(l, V_FFN_POST, n)

        def load_x(t0, n):
            S.phase = "load_x"
            NT = n // 128
            for t in range(NT):
                S.dma("sp", f"xl{t}", stg.v(t * D, (t + 1) * D).ap, x_d[t0 + t * 128: t0 + (t + 1) * 128, :],
                      writes=[stg.v(t * D, (t + 1) * D)])
            for c in range(16):
                pb = nb()
                for t in range(NT):
                    tr(pb.v(t * 128, (t + 1) * 128), stg.v(t * D + c * 128, t * D + (c + 1) * 128))
                cp("act" if c % 2 else "dve", chunk(X, c, n), pb.v(0, n))

        def store_x(t0, n):
            S.phase = "store_x"
            NT = n // 128
            for t in range(NT):
                for q in range(4):
                    pb = nb()
                    for cc in range(4):
                        c = q * 4 + cc
                        tr(pb.v(cc * 128, (cc + 1) * 128), X.v(c * NMAX + t * 128, c * NMAX + (t + 1) * 128))
                    cp("act" if q % 2 else "dve", stg.v(t * D + q * 512, t * D + (q + 1) * 512), pb.v(0, 512))
                S.dma("sp", f"ys{t}", y_d[t0 + t * 128: t0 + (t + 1) * 128, :], stg.v(t * D, (t + 1) * D).ap,
                      reads=[stg.v(t * D, (t + 1) * D)])

        t0 = 0
        for gi, n in enumerate(groups):
            load_x(t0, n)
            for l in range(L):
                if stage >= 1:
                    mixer(l, n, gi)
                if stage >= 2:
                    ffn(l, n)
            store_x(t0, n)
            t0 += n
        for eng in ("sp", "act", "dve", "pool"):
            S.wait_all(eng)
        S._emit_waits("pe", {s: v for s, v in S.cnt.items() if v > 0 and s != "pe"})
        build.pe_labels = S.pe_labels
        build.stats = dict(ninst=S.ninst, nwaits=S.nwaits, sbuf=total, cnt=dict(S.cnt))
    return nc


def _fm(v):
    return np.swapaxes(v.reshape(v.shape[:-1] + (16, 128)), -1, -2)


def _prep_params(inp, layers):
    L = len(layers)
    f = lambda a: np.asarray(a, dtype=np.float32)
    vecs = []
    for l in layers:
        vs = [f(inp["norm_mix_pre"])[l], f(inp["norm_mix_post"])[l], f(inp["norm_ffn_pre"])[l], f(inp["norm_ffn_post"])[l],
              f(inp["conv_b"])[l], f(inp["conv_ln_g"])[l], f(inp["conv_ln_b"])[l], f(inp["b_gate"])[l, 0], f(inp["b_gate"])[l, 1]]
        vecs.append(np.stack([_fm(v) for v in vs], axis=0))
    pv = np.stack(vecs, 0)
    pv = np.ascontiguousarray(pv.transpose(2, 0, 1, 3).reshape(128, L * NV * 16))
    cw = f(inp["conv_w"])[layers]
    cw = cw.reshape(L, CW, 16, 128).transpose(0, 3, 2, 1)
    cw = np.ascontiguousarray(cw.reshape(L, 128, 16 * CW))
    g = f(inp["sgu_ln_g"])[layers]
    b = f(inp["sgu_ln_b"])[layers]
    gb = np.concatenate([g, b], axis=-1)[:, None, :]
    gb = np.ascontiguousarray(np.broadcast_to(gb, (L, 128, 2 * D)))
    ws = f(inp["w_spatial"])[layers]
    wst = np.ascontiguousarray(ws.transpose(3, 0, 1, 2).reshape(128, L * 8 * 128))
    bsp = np.ascontiguousarray(f(inp["b_spatial"])[layers].reshape(1, L * 8 * 128))
    out = dict(pv=pv, cw=cw, gb=gb, wst=wst, bsp=bsp)
    for k_in, k_out in (("w_in", "w_in"), ("w_a_out", "w_a"), ("w_b_out", "w_b"), ("w_o", "w_o"), ("w_gate_up", "w_gu"), ("w_down", "w_d")):
        w = f(inp[k_in])
        out[k_out] = w if (L == w.shape[0] and list(layers) == list(range(L))) else np.ascontiguousarray(w[layers])
    return out


def _core_x(x, core):
    b, half = divmod(core, 2)
    xc = np.zeros((TOK, D), dtype=np.float32)
    if half == 0:
        xc[HALO:] = x[b, :SEQ // 2]
    else:
        xc[:] = x[b, SEQ // 2 - HALO:]
    hm = np.full((128, 1), float(half), dtype=np.float32)
    return xc, hm


_NC_CACHE = {}


def _get_nc(L):
    if L not in _NC_CACHE:
        _NC_CACHE[L] = build(L)
    return _NC_CACHE[L]


FUSED = True


def kernel(**inputs):
    x = np.asarray(inputs["x"], dtype=np.float32)
    cores = list(range(NCORES))
    xs = [_core_x(x, c) for c in cores]
    if FUSED:
        nc = _get_nc(DEPTH)
        p = _prep_params(inputs, list(range(DEPTH)))
        in_maps = [dict(p, x=xc, hm=hm) for (xc, hm) in xs]
        res = run_bass_kernel_spmd(nc, in_maps, core_ids=cores)
        ys = [np.asarray(r["y"]) for r in res.results]
    else:
        nc = _get_nc(1)
        cur = [xc for (xc, _) in xs]
        for l in range(DEPTH):
            p = _prep_params(inputs, [l])
            in_maps = [dict(p, x=cur[c], hm=xs[c][1]) for c in cores]
            res = run_bass_kernel_spmd(nc, in_maps, core_ids=cores)
            cur = [np.ascontiguousarray(np.asarray(r["y"], dtype=np.float32)) for r in res.results]
        ys = cur
    out = np.empty((BATCH, SEQ, D), dtype=np.float32)
    for c in cores:
        b, half = divmod(c, 2)
        out[b, half * (SEQ // 2):(half + 1) * (SEQ // 2)] = ys[c][HALO:]
    return out
```

```python
from contextlib import ExitStack

import numpy as np
import concourse.bass as bass
import concourse.mybir as mybir
from concourse.bass_utils import run_bass_kernel_spmd

F32 = mybir.dt.float32
BF16 = mybir.dt.bfloat16
AF = mybir.ActivationFunctionType
ALU = mybir.AluOpType

D = 2048
BATCH = 4
SEQ = 4096
DEPTH = 4
FF = 5632
NJ = FF // 128
CW = 31
EPS = 1e-6
NCORES = 8
HALO = 256
TOK = SEQ // 2 + HALO
GROUPS = [512, 512, 512, 512, HALO]
NMAX = 512
NV = 9
(V_MIX_PRE, V_MIX_POST, V_FFN_PRE, V_FFN_POST, V_CONV_B, V_CLN_G, V_CLN_B, V_BG0, V_BG1) = range(NV)
NSLOT = 4
NPE = 15
WBLK = 16 * 256


class View:
    __slots__ = ("ap", "reg")

    def __init__(self, ap, reg):
        self.ap = ap
        self.reg = reg


class Buf:
    def __init__(self, base, ap2d, off, esize, ncols, tracked=True):
        self.base = base
        self.a = ap2d
        self.off = off
        self.es = esize
        self.ncols = ncols
        self.tracked = tracked

    def v(self, lo=0, hi=None, p0=None, p1=None):
        hi = self.ncols if hi is None else hi
        ap = self.a[:, lo:hi] if p0 is None else self.a[p0:p1, lo:hi]
        reg = (self.base, self.off + lo * self.es, self.off + hi * self.es) if self.tracked else None
        return View(ap, reg)


class Sched:
    ENG = ("pe", "act", "dve", "pool", "sp")

    def __init__(self, nc, stack):
        self.nc = nc
        self.stack = stack
        self.e = {"pe": nc.tensor, "act": nc.scalar, "dve": nc.vector, "pool": nc.gpsimd, "sp": nc.sync}
        self.semh = {}
        self.cnt = {}
        for n in ("pe", "act", "dve", "pool"):
            self.new_sem(n)
        self.waited = {n: {} for n in self.ENG}
        self.recs = {}
        self.ninst = 0
        self.nwaits = 0
        self.last_pe = None
        self.phase = "setup"
        self.pe_labels = []

    def new_sem(self, name):
        self.semh[name] = self.stack.enter_context(self.nc.semaphore(name))
        self.cnt[name] = 0
        return name

    def _collect(self, reads, writes):
        need = {}
        for r in reads:
            if r is None:
                continue
            base, lo, hi = r
            for rec in self.recs.get(base, ()):
                if rec[2] and rec[0] < hi and lo < rec[1]:
                    s, v = rec[3]
                    if need.get(s, 0) < v:
                        need[s] = v
        for w in writes:
            if w is None:
                continue
            base, lo, hi = w
            for rec in self.recs.get(base, ()):
                if rec[0] < hi and lo < rec[1]:
                    s, v = rec[3]
                    if need.get(s, 0) < v:
                        need[s] = v
        return need

    def _record(self, reads, writes, tok):
        for w in writes:
            if w is None:
                continue
            base, lo, hi = w
            lst = self.recs.setdefault(base, [])
            lst[:] = [r for r in lst if not (lo <= r[0] and r[1] <= hi)]
            lst.append([lo, hi, True, tok])
        for r in reads:
            if r is None:
                continue
            base, lo, hi = r
            lst = self.recs.setdefault(base, [])
            for rec in lst:
                if (not rec[2]) and rec[0] == lo and rec[1] == hi and rec[3][0] == tok[0]:
                    if rec[3][1] < tok[1]:
                        rec[3] = tok
                    break
            else:
                lst.append([lo, hi, False, tok])

    def _emit_waits(self, eng, need):
        w = self.waited[eng]
        for s, v in need.items():
            if eng == "pe" and s == "pe":
                continue
            if s == "pe" and v > self.cnt["pe"]:
                assert v == self.cnt["pe"] + 1 and self.last_pe is not None and not self.last_pe[1]
                self.last_pe[0].then_inc(self.semh["pe"], 1)
                self.last_pe[1] = True
                self.cnt["pe"] += 1
            if w.get(s, 0) < v:
                self.e[eng].wait_ge(self.semh[s], v)
                w[s] = v
                self.nwaits += 1

    def op(self, eng, fn, reads=(), writes=(), inc=True):
        reads = [r.reg for r in reads]
        writes = [w.reg for w in writes]
        self._emit_waits(eng, self._collect(reads, writes))
        ins = fn(self.e[eng])
        self.ninst += 1
        if eng == "pe":
            self.last_pe = [ins, bool(inc)]
            self.pe_labels.append(self.phase)
        if inc:
            self.cnt[eng] += 1
            ins.then_inc(self.semh[eng], 1)
            tok = (eng, self.cnt[eng])
        else:
            tok = (eng, self.cnt[eng] + 1)
        self._record(reads, writes, tok)
        return tok

    def dma(self, eng, sem, out, in_, reads=(), writes=(), **kw):
        reads = [r if (r is None or isinstance(r, tuple)) else r.reg for r in reads]
        writes = [w if (w is None or isinstance(w, tuple)) else w.reg for w in writes]
        self._emit_waits(eng, self._collect(reads, writes))
        ins = self.e[eng].dma_start(out=out, in_=in_, **kw)
        self.ninst += 1
        self.cnt[sem] += 16
        ins.then_inc(self.semh[sem], 16)
        tok = (sem, self.cnt[sem])
        self._record(reads, writes, tok)
        return tok

    def wait_all(self, eng):
        self._emit_waits(eng, {s: v for s, v in self.cnt.items() if v > 0})


def build(L, groups=GROUPS, stage=9):
    nc = bass.Bass("TRN2", target_bir_lowering=False)
    ntok = sum(groups)
    dr = {}

    def din(name, shape):
        dr[name] = nc.dram_tensor(name, shape, F32, kind="ExternalInput").ap()
        return dr[name]

    x_d = din("x", [ntok, D])
    hm_d = din("hm", [128, 1])
    pv_d = din("pv", [128, L * NV * 16])
    cw_d = din("cw", [L, 128, 16 * CW])
    gb_d = din("gb", [L, 128, 2 * D])
    wst_d = din("wst", [128, L * 8 * 128])
    bsp_d = din("bsp", [1, L * 8 * 128])
    w_in = din("w_in", [L, D, 6 * D])
    w_a = din("w_a", [L, D, D])
    w_b = din("w_b", [L, D, D])
    w_o = din("w_o", [L, D, D])
    w_gu = din("w_gu", [L, D, 2 * FF])
    w_d = din("w_d", [L, FF, D])
    y_d = nc.dram_tensor("y", [ntok, D], F32, kind="ExternalOutput").ap()

    MATS = {"win": (w_in, 48), "wa": (w_a, 8), "wb": (w_b, 8), "wo": (w_o, 8), "wgu": (w_gu, 44), "wd": (w_d, 24)}
    sc = {}
    for l in range(L):
        for mname, (_, nb_) in MATS.items():
            sc[(mname, l)] = nc.dram_tensor(f"sc_{mname}_{l}", [nb_, 128, WBLK], BF16, kind="Internal").ap()

    with ExitStack() as st:
        S = Sched(nc, st)

        plan = []
        off = [0]
        bufs = {}

        def alloc(name, dt, es, ncols, tracked=True):
            o = off[0]
            plan.append((name, dt, es, ncols, tracked, o))
            off[0] = o + ((es * ncols + 63) // 64) * 64
            return o

        alloc("x", F32, 4, 16 * NMAX)
        o_hA = alloc("h", BF16, 2, 16 * NMAX)
        alloc("A", BF16, 2, 16 * NMAX)
        o_uvu = alloc("UB", BF16, 2, 16 * NMAX)
        alloc("VM", BF16, 2, 16 * NMAX)
        alloc("U", BF16, 2, 16 * NMAX)
        for i in range(NSLOT):
            alloc(f"W{i}", BF16, 2, WBLK)
        alloc("GB", F32, 4, 2 * D)
        for i in range(6):
            alloc(f"T{i}", F32, 4, NMAX)
        for i in range(3):
            alloc(f"hat{i}", BF16, 2, NMAX + 32)
        for i in range(12):
            alloc(f"dg{i}", BF16, 2, 128)
        alloc("identb", BF16, 2, 128)
        for i in range(4):
            alloc(f"sq{i}", BF16, 2, NMAX)
        alloc("pv", F32, 4, L * NV * 16, False)
        alloc("cw", F32, 4, 16 * CW)
        alloc("state", BF16, 2, L * 16 * 30)
        alloc("wsT", BF16, 2, L * 8 * 128)
        alloc("bspT", BF16, 2, L * 8 * 128)
        alloc("ones", BF16, 2, 128)
        alloc("ident", F32, 4, 128)
        alloc("hm", F32, 4, 1, False)
        alloc("vs1", F32, 4, 32)
        alloc("vs2", F32, 4, 32)
        alloc("vst", F32, 4, 32)
        total = off[0]
        arena = st.enter_context(nc.sbuf_tensor("arena", [128, total // 2], BF16))
        a_bf = arena[:, :]
        a_f32 = a_bf.bitcast(F32)
        for (name, dt, es, ncols, tracked, o) in plan:
            base_ap = a_f32 if dt == F32 else a_bf
            bufs[name] = Buf("arena", base_ap[:, o // es: o // es + ncols], o, es, ncols, tracked)
        bufs["Y"] = Buf("arena", a_f32[:, o_hA // 4: o_hA // 4 + 16 * NMAX], o_hA, 4, 16 * NMAX)
        bufs["C"] = Buf("arena", a_bf[:, o_uvu // 2: o_uvu // 2 + 16 * NMAX], o_uvu, 2, 16 * NMAX)
        bufs["STG"] = Buf("arena", a_f32[:, o_uvu // 4: o_uvu // 4 + 24 * NMAX], o_uvu, 4, 24 * NMAX)
        bufs["F"] = Buf("arena", a_bf[:, o_uvu // 2: o_uvu // 2 + NJ * NMAX], o_uvu, 2, NJ * NMAX)
        B = bufs
        X, H, A_, UB, VM, U, GB, Y, C, Fh = B["x"], B["h"], B["A"], B["UB"], B["VM"], B["U"], B["GB"], B["Y"], B["C"], B["F"]
        T = [B[f"T{i}"] for i in range(6)]
        HAT = [B[f"hat{i}"] for i in range(3)]
        SQ = [B[f"sq{i}"] for i in range(4)]
        PV, CWB, STATE, WST, BSPT, ONES, IDENT, HM = B["pv"], B["cw"], B["state"], B["wsT"], B["bspT"], B["ones"], B["ident"], B["hm"]
        VS1, VS2, VST = B["vs1"], B["vs2"], B["vst"]
        DG = [B[f"dg{i}"] for i in range(12)]
        IDENTB = B["identb"]
        rot_dg = [0]

        banks = []
        for i in range(8):
            h_ = st.enter_context(nc.psum_tensor(f"pb{i}", [128, 512], F32))
            banks.append(Buf(f"pb{i}", h_[:, :], 0, 4, 512))
        rot = {"b": 0, "t": 0, "q": 0, "h": 0}

        def nb():
            b = banks[rot["b"] % 4]
            rot["b"] += 1
            return b

        def nsq():
            b = SQ[rot["q"] % 4]
            rot["q"] += 1
            return b

        SB1, SB2 = banks[6], banks[7]
        CB = [banks[4], banks[5]]

        for s_ in ("cst", "gbs", "cws") + tuple(f"xl{i}" for i in range(4)) + tuple(f"ys{i}" for i in range(4)):
            S.new_sem(s_)
        for i in range(NSLOT):
            S.new_sem(f"ws{i}")
        for i in range(NSLOT):
            S.new_sem(f"wb{i}")
            S.new_sem(f"wp{i}")

        def act(out, in_, func, bias=None, scale=None, accum=None, extra=()):
            kw = {}
            if bias is not None:
                kw["bias"] = bias
            if scale is not None:
                kw["scale"] = scale
            if accum is not None:
                kw["accum_out"] = accum.ap
            w = [out] + ([accum] if accum is not None else [])
            return S.op("act", lambda e: e.activation(out.ap, in_.ap, func, **kw), reads=[in_] + list(extra), writes=w)

        def tt(eng, out, a, b, op):
            return S.op(eng, lambda e: e.tensor_tensor(out.ap, a.ap, b.ap, op), reads=[a, b], writes=[out])

        def stt(out, in0, scalar, in1, op0, op1, extra=()):
            return S.op("dve", lambda e: e.scalar_tensor_tensor(out.ap, in0.ap, scalar, in1.ap, op0, op1),
                        reads=[in0, in1] + list(extra), writes=[out])

        def ts(eng, out, in0, s1, s2, op0, op1=None, extra=()):
            if op1 is None:
                return S.op(eng, lambda e: e.tensor_scalar(out.ap, in0.ap, s1, None, op0), reads=[in0] + list(extra), writes=[out])
            return S.op(eng, lambda e: e.tensor_scalar(out.ap, in0.ap, s1, s2, op0, op1), reads=[in0] + list(extra), writes=[out])

        def rsqrt_(v):
            S.op("dve", lambda e: e.reciprocal(v.ap, v.ap), reads=[v], writes=[v])
            act(v, v, AF.Sqrt)

        def cp(eng, out, in_):
            if eng == "act":
                return S.op("act", lambda e: e.copy(out.ap, in_.ap), reads=[in_], writes=[out])
            return S.op(eng, lambda e: e.tensor_copy(out.ap, in_.ap), reads=[in_], writes=[out])

        def mm(out, lhsT, rhs, start, stop, inc=None):
            return S.op("pe", lambda e: e.matmul(out.ap, lhsT.ap, rhs.ap, start=start, stop=stop),
                        reads=[lhsT, rhs], writes=[out], inc=(stop if inc is None else inc))

        def tr(out, in_):
            return S.op("pe", lambda e: e.transpose(out.ap, in_.ap, IDENT.v().ap), reads=[in_], writes=[out])

        deferred = []

        def flush():
            while deferred:
                deferred.pop(0)()

        def pvs(l, vec, c):
            col = (l * NV + vec) * 16 + c
            return PV.v(col, col + 1).ap

        def convert_layer(l):
            for mname, (wsrc, nblk) in MATS.items():
                sem = f"cv_{mname}_{l}"
                for b in range(nblk):
                    if mname == "wd":
                        cb, kr = divmod(b, 3)
                        kk = 16 if kr < 2 else 12
                        src = wsrc[l, kr * 2048: kr * 2048 + kk * 128, cb * 256:(cb + 1) * 256]
                    else:
                        kk = 16
                        src = wsrc[l, :, b * 256:(b + 1) * 256]
                    src = src.rearrange("(k p) c -> p k c", p=128)
                    dst = sc[(mname, l)][b][:, 0:kk * 256].rearrange("p (k c) -> p k c", k=kk)
                    S.dma("pool", sem, dst, src, writes=[(f"sc_{mname}_{l}", 0, 1)])

        def layer_blocks(l):
            seq = []
            for j in range(8):
                seq += [("win", l, 8 + j, 16), ("win", l, j, 16), ("win", l, 24 + j, 16)]
            for j in range(8):
                seq.append(("win", l, 16 + j, 16))
            for j in range(8):
                seq += [("win", l, 32 + j, 16), ("win", l, 40 + j, 16), ("wa", l, j, 16), ("wb", l, j, 16)]
            for j in range(8):
                seq.append(("wo", l, j, 16))
            for j in range(22):
                seq += [("wgu", l, j, 16), ("wgu", l, 22 + j, 16)]
            for cb in range(8):
                for kr in range(3):
                    seq.append(("wd", l, cb * 3 + kr, 16 if kr < 2 else 12))
            return seq

        wseq = []
        for _g in groups:
            for l in range(L):
                lb = layer_blocks(l)
                if stage == 1:
                    lb = [b_ for b_ in lb if b_[0] not in ("wgu", "wd")]
                if stage >= 1:
                    wseq += lb
        wst_ = {"issued": 0, "next": 0}
        WS = [B[f"W{i}"] for i in range(NSLOT)]

        first_pass = len(layer_blocks(0)) * L

        def w_src(mname, l, b, kk):
            wsrc = MATS[mname][0]
            if mname == "wd":
                cb, kr = divmod(b, 3)
                src = wsrc[l, kr * 2048: kr * 2048 + kk * 128, cb * 256:(cb + 1) * 256]
            else:
                src = wsrc[l, :, b * 256:(b + 1) * 256]
            return src.rearrange("(k p) c -> p k c", p=128)

        def w_issue():
            i = wst_["issued"]
            if i >= len(wseq):
                return
            mname, l, b, kk = wseq[i]
            sl = i % NSLOT
            slot = WS[sl].v(0, kk * 256)
            blk = (f"sc_{mname}_{l}", b, b + 1)
            if i < first_pass:
                S.dma("pool", f"wp{sl}", slot.ap.rearrange("p (k c) -> p k c", k=kk), w_src(mname, l, b, kk), writes=[slot])
                S.dma("sp", f"wb{sl}", sc[(mname, l)][b][:, 0:kk * 256], slot.ap, reads=[slot], writes=[blk])
            else:
                S.dma("sp", f"ws{sl}", slot.ap, sc[(mname, l)][b][:, 0:kk * 256], reads=[blk], writes=[slot])
            wst_["issued"] = i + 1

        def wget(desc):
            i = wst_["next"]
            assert wseq[i][:3] == desc, (wseq[i], desc)
            while wst_["issued"] < min(i + NSLOT, len(wseq)):
                w_issue()
            wst_["next"] = i + 1
            return WS[i % NSLOT]

        stg = B["STG"]
        LW = L * 1024
        o_b0, o_b1 = LW, 2 * LW
        S.dma("sp", "cst", PV.v().ap, pv_d, writes=[])
        S.dma("sp", "cst", HM.v().ap, hm_d, writes=[])
        S.dma("sp", "cst", stg.v(0, LW).ap, wst_d, writes=[stg.v(0, LW)])
        S.dma("sp", "cst", stg.v(o_b0, o_b0 + LW, 0, 1).ap, bsp_d, writes=[stg.v(o_b0, o_b0 + LW)])
        cst_all = {"cst": S.cnt["cst"]}
        S._emit_waits("pool", cst_all)
        S._emit_waits("dve", cst_all)
        S.op("pool", lambda e: e.memset(ONES.v().ap, 1.0), writes=[ONES.v()])
        S.op("pool", lambda e: e.memset(IDENT.v().ap, 1.0), writes=[IDENT.v()])
        S.op("pool", lambda e: e.memset(STATE.v().ap, 0.0), writes=[STATE.v()])
        S.op("pool", lambda e: e.affine_select(out=IDENT.v().ap, in_=IDENT.v().ap, pattern=[[-1, 128]],
                                               compare_op=ALU.is_equal, fill=0.0, base=0, channel_multiplier=1),
             reads=[IDENT.v()], writes=[IDENT.v()])
        S.op("pool", lambda e: e.tensor_copy(IDENTB.v().ap, IDENT.v().ap), reads=[IDENT.v()], writes=[IDENTB.v()])
        S.op("pool", lambda e: e.affine_select(out=stg.v(0, LW).ap, in_=stg.v(0, LW).ap,
                                               pattern=[[0, L * 8], [1, 128]], compare_op=ALU.is_ge, fill=0.0,
                                               base=0, channel_multiplier=-1),
             reads=[stg.v(0, LW)], writes=[stg.v(0, LW)])
        S.op("pool", lambda e: e.tensor_copy(WST.v().ap, stg.v(0, LW).ap), reads=[stg.v(0, LW)], writes=[WST.v()])
        bs0 = stg.v(o_b0, o_b0 + LW)
        bs1 = stg.v(o_b1, o_b1 + LW)
        r0 = lambda bf, o: bf.v(o, o + LW, 0, 1).ap
        lo_bf = Fh.v(2 * LW, 3 * LW)
        S.op("dve", lambda e: e.tensor_copy(r0(BSPT, 0), r0(stg, o_b0)), reads=[bs0], writes=[BSPT.v()])
        S.op("dve", lambda e: e.tensor_copy(r0(stg, o_b1), r0(BSPT, 0)), reads=[BSPT.v()], writes=[bs1])
        S.op("dve", lambda e: e.tensor_tensor(r0(stg, o_b1), r0(stg, o_b0), r0(stg, o_b1), ALU.subtract),
             reads=[bs0, bs1], writes=[bs1])
        S.op("dve", lambda e: e.tensor_copy(r0(Fh, 2 * LW), r0(stg, o_b1)), reads=[bs1, bs0], writes=[lo_bf])
        S.dma("sp", "cst", BSPT.v(0, LW, 1, 2).ap, r0(Fh, 2 * LW), reads=[lo_bf], writes=[BSPT.v()])
        setup_need = {s_: v for s_, v in S.cnt.items() if v > 0 and not s_.startswith("cv_")}
        for eng in ("pe", "act", "dve", "pool"):
            S._emit_waits(eng, dict(setup_need))

        def chunk(buf, c, n, width=NMAX):
            return buf.v(c * width, c * width + n)

        def load_layer_consts(l):
            S.dma("sp", "gbs", GB.v().ap, gb_d[l], writes=[GB.v()])
            S.dma("sp", "cws", CWB.v().ap, cw_d[l], writes=[CWB.v()])

        def rms_to_h(l, vec, n):
            S.phase = "rms_pre"
            for c in range(16):
                q = nsq().v(0, n)
                act(q, chunk(X, c, n), AF.Square)
                mm(SB1.v(0, n), ONES.v(), q, c == 0, c == 15)
            r = T[4].v(0, n)
            ts("dve", r, SB1.v(0, n), 1.0 / D, EPS, ALU.mult, ALU.add)
            rsqrt_(r)
            for c in range(16):
                stt(chunk(H, c, n), chunk(X, c, n), pvs(l, vec, c), r, ALU.mult, ALU.mult)

        def post_norm_residual(l, vec, n):
            r = T[4].v(0, n)
            ts("dve", r, SB1.v(0, n), 1.0 / D, EPS, ALU.mult, ALU.add)
            rsqrt_(r)
            for m in range(16):
                ym = chunk(Y, m, n)
                stt(ym, ym, pvs(l, vec, m), r, ALU.mult, ALU.mult)
                tt("dve", chunk(X, m, n), chunk(X, m, n), ym, ALU.add)

        def proj_fm(wslot, mm_, src, n, nk=16, k0=0, pb=None, start=True, stop=True):
            if pb is None:
                pb = nb()
            for k in range(nk):
                mm(pb.v(0, n), wslot.v(k * 256 + mm_ * 128, k * 256 + (mm_ + 1) * 128), chunk(src, k0 + k, n),
                   start and k == 0, stop and k == nk - 1)
            return pb

        def mixer(l, n, gi):
            NT = n // 128
            load_layer_consts(l)
            rms_to_h(l, V_MIX_PRE, n)
            for j in range(8):
                S.phase = "a_gate"
                wg = wget(("win", l, 8 + j))
                sg = []
                for mm_ in range(2):
                    pb = proj_fm(wg, mm_, H, n)
                    t_ = T[(2 * j + mm_) % 4].v(0, n)
                    act(t_, pb.v(0, n), AF.Sigmoid)
                    sg.append(t_)
                S.phase = "a_in"
                wa = wget(("win", l, j))
                pbs = [proj_fm(wa, mm_, H, n) for mm_ in range(2)]
                S.phase = "a_fin_stats"
                flush()
                hats = []
                for mm_ in range(2):
                    m = 2 * j + mm_
                    hat = HAT[rot["h"] % 3]
                    rot["h"] += 1
                    sto = (l * 16 + m) * 30
                    cp("dve", hat.v(0, 30), STATE.v(sto, sto + 30))
                    tt("dve", hat.v(30, 30 + n), pbs[mm_].v(0, n), sg[mm_], ALU.mult)
                    if gi == 0:
                        ts("dve", hat.v(30, 30 + HALO), hat.v(30, 30 + HALO), HM.v().ap, None, ALU.mult)
                    cp("dve", STATE.v(sto, sto + 30), hat.v(n, n + 30))
                    hats.append(hat)
                S.phase = "conv"
                accs = []
                for mm_ in range(2):
                    m = 2 * j + mm_
                    acc = CB[m % 2].v(0, n)
                    accs.append(acc)
                    cwc = m * CW
                    for k in range(NPE):
                        dg = DG[rot_dg[0] % 12].v()
                        rot_dg[0] += 1
                        S.op("dve", lambda e: e.tensor_scalar(dg.ap, IDENTB.v().ap, CWB.v(cwc + k, cwc + k + 1).ap, None, ALU.mult),
                             reads=[CWB.v(cwc + k, cwc + k + 1)], writes=[dg])
                        mm(acc, dg, hats[mm_].v(k, k + n), k == 0, k == NPE - 1)
                for mm_ in range(2):
                    m = 2 * j + mm_
                    acc = accs[mm_]
                    cwc = m * CW
                    for k in range(NPE, CW):
                        stt(acc, hats[mm_].v(k, k + n), CWB.v(cwc + k, cwc + k + 1).ap, acc, ALU.mult, ALU.add)

                    def fin(m=m, acc=acc):
                        cm = chunk(C, m, n)
                        act(cm, acc, AF.Identity, bias=pvs(l, V_CONV_B, m))
                        q2 = nsq().v(0, n)
                        act(q2, acc, AF.Square, bias=pvs(l, V_CONV_B, m))
                        mm(SB1.v(0, n), ONES.v(), cm, m == 0, m == 15)
                        mm(SB2.v(0, n), ONES.v(), q2, m == 0, m == 15)
                    deferred.append(fin)
                S.phase = "v_proj"
                wv = wget(("win", l, 24 + j))
                for t in range(NT):
                    pb = nb()
                    for k in range(16):
                        mm(pb.v(0, 256), H.v(k * NMAX + t * 128, k * NMAX + (t + 1) * 128), wv.v(k * 256, (k + 1) * 256),
                           k == 0, k == 15)
                    vb = VM.v(t * D + j * 256, t * D + (j + 1) * 256)
                    act(vb, pb.v(0, 256), AF.Gelu, accum=VS1.v(t * 8 + j, t * 8 + j + 1))
                    act(nsq().v(0, 256), vb, AF.Square, accum=VS2.v(t * 8 + j, t * 8 + j + 1))
            S.phase = "a_fin_last"
            flush()
            mean = T[4].v(0, n)
            rstd = T[5].v(0, n)
            tmp = T[0].v(0, n)
            ts("dve", mean, SB1.v(0, n), 1.0 / D, None, ALU.mult)
            tt("dve", tmp, mean, mean, ALU.mult)
            stt(rstd, SB2.v(0, n), 1.0 / D, tmp, ALU.mult, ALU.subtract)
            ts("dve", rstd, rstd, EPS, None, ALU.add)
            rsqrt_(rstd)
            stt(mean, mean, -1.0, rstd, ALU.mult, ALU.mult)
            s1, s2 = VST.v(12, 12 + NT), VST.v(16, 16 + NT)
            mean_t, msq_t, rstd_t = VST.v(0, NT), VST.v(4, 4 + NT), VST.v(8, 8 + NT)
            S.op("dve", lambda e: e.tensor_reduce(s1.ap, VS1.v(0, NT * 8).ap.rearrange("p (t j) -> p t j", j=8),
                                                  mybir.AxisListType.X, ALU.add), reads=[VS1.v(0, NT * 8)], writes=[s1])
            S.op("dve", lambda e: e.tensor_reduce(s2.ap, VS2.v(0, NT * 8).ap.rearrange("p (t j) -> p t j", j=8),
                                                  mybir.AxisListType.X, ALU.add), reads=[VS2.v(0, NT * 8)], writes=[s2])
            ts("dve", mean_t, s1, 1.0 / D, None, ALU.mult)
            tt("dve", msq_t, mean_t, mean_t, ALU.mult)
            stt(rstd_t, s2, 1.0 / D, msq_t, ALU.mult, ALU.subtract)
            ts("dve", rstd_t, rstd_t, EPS, None, ALU.add)
            rsqrt_(rstd_t)
            nmr_t = VST.v(20, 20 + NT)
            stt(nmr_t, mean_t, -1.0, rstd_t, ALU.mult, ALU.mult)
            tasks = []
            rot_t = [0]

            def v_task(t, q):
                vb = VM.v(t * D + q * 512, t * D + (q + 1) * 512)
                tmp = T[rot_t[0] % 4].v(0, 512)
                rot_t[0] += 1
                act(tmp, vb, AF.Identity, bias=VST.v(20 + t, 21 + t).ap, scale=VST.v(8 + t, 9 + t).ap,
                    extra=[nmr_t, rstd_t])
                tt("dve", tmp, tmp, GB.v(q * 512, (q + 1) * 512), ALU.mult)
                tt("dve", vb, tmp, GB.v(D + q * 512, D + (q + 1) * 512), ALU.add)

            def a_task(m):
                t_ = T[rot_t[0] % 4].v(0, n)
                rot_t[0] += 1
                tt("dve", t_, chunk(C, m, n), rstd, ALU.mult)
                tt("dve", t_, t_, mean, ALU.add)
                act(chunk(A_, m, n), t_, AF.Silu, bias=pvs(l, V_CLN_B, m), scale=pvs(l, V_CLN_G, m))

            for t in range(NT):
                for q in range(4):
                    tasks.append(lambda t=t, q=q: v_task(t, q))
            for m in range(16):
                tasks.append(lambda m=m: a_task(m))
            S.phase = "u_proj"
            for j in range(8):
                wu = wget(("win", l, 16 + j))
                for mm_ in range(2):
                    pb = proj_fm(wu, mm_, H, n)
                    act(chunk(U, 2 * j + mm_, n), pb.v(0, n), AF.Gelu)
                    for _ in range(2):
                        if tasks:
                            tasks.pop(0)()
            while tasks:
                tasks.pop(0)()
            S.phase = "spatial"
            for m in range(16):
                g8 = l * 8 + m // 2
                pb = nb()
                for t in range(NT):
                    o = pb.v(t * 128, (t + 1) * 128)
                    mm(o, VM.v(t * D + m * 128, t * D + (m + 1) * 128), WST.v(g8 * 128, (g8 + 1) * 128), True, False, inc=False)
                    mm(o, ONES.v(0, 128, 0, 2), BSPT.v(g8 * 128, (g8 + 1) * 128, 0, 2), False, True, inc=(t == NT - 1))
                tt("dve", chunk(UB, m, n), pb.v(0, n), chunk(U, m, n), ALU.mult)
            S.phase = "merge"
            for j in range(8):
                g0 = []
                w0 = wget(("win", l, 32 + j))
                for mm_ in range(2):
                    pb = proj_fm(w0, mm_, H, n)
                    t_ = T[mm_].v(0, n)
                    act(t_, pb.v(0, n), AF.Sigmoid, bias=pvs(l, V_BG0, 2 * j + mm_))
                    g0.append(t_)
                g1 = []
                w1 = wget(("win", l, 40 + j))
                for mm_ in range(2):
                    pb = proj_fm(w1, mm_, H, n)
                    t_ = T[2 + mm_].v(0, n)
                    act(t_, pb.v(0, n), AF.Sigmoid, bias=pvs(l, V_BG1, 2 * j + mm_))
                    g1.append(t_)
                wa_ = wget(("wa", l, j))
                for mm_ in range(2):
                    pb = proj_fm(wa_, mm_, A_, n)
                    tt("dve", g0[mm_], pb.v(0, n), g0[mm_], ALU.mult)
                wb_ = wget(("wb", l, j))
                for mm_ in range(2):
                    pb = proj_fm(wb_, mm_, UB, n)
                    tt("dve", g1[mm_], pb.v(0, n), g1[mm_], ALU.mult)
                    tt("dve", chunk(VM, 2 * j + mm_, n), g0[mm_], g1[mm_], ALU.add)
            S.phase = "wo"
            for j in range(8):
                wo_ = wget(("wo", l, j))
                for mm_ in range(2):
                    m = 2 * j + mm_
                    pb = proj_fm(wo_, mm_, VM, n)
                    flush()
                    cp("act", chunk(Y, m, n), pb.v(0, n))
                    q = nsq().v(0, n)
                    act(q, pb.v(0, n), AF.Square)
                    deferred.append(lambda m=m, q=q: mm(SB1.v(0, n), ONES.v(), q, m == 0, m == 15))
            flush()
            post_norm_residual(l, V_MIX_POST, n)

        def ffn(l, n):
            rms_to_h(l, V_FFN_PRE, n)
            S.phase = "gate_up"
            for j in range(22):
                wg = wget(("wgu", l, j))
                sl = []
                for mm_ in range(2):
                    pb = proj_fm(wg, mm_, H, n)
                    t_ = T[(2 * j + mm_) % 4].v(0, n)
                    act(t_, pb.v(0, n), AF.Silu)
                    sl.append(t_)
                wu = wget(("wgu", l, 22 + j))
                for mm_ in range(2):
                    pb = proj_fm(wu, mm_, H, n)
                    tt("dve", chunk(Fh, 2 * j + mm_, n), pb.v(0, n), sl[mm_], ALU.mult)
            S.phase = "down"
            for cb in range(8):
                pbs = [nb(), nb()]
                for kr in range(3):
                    kk = 16 if kr < 2 else 12
                    wd_ = wget(("wd", l, cb * 3 + kr))
                    for mm_ in range(2):
                        proj_fm(wd_, mm_, Fh, n, nk=kk, k0=kr * 16, pb=pbs[mm_], start=(kr == 0), stop=(kr == 2))
                flush()
                for mm_ in range(2):
                    m = 2 * cb + mm_
                    cp("act", chunk(Y, m, n), pbs[mm_].v(0, n))
                    q = nsq().v(0, n)
                    act(q, pbs[mm_].v(0, n), AF.Square)
                    deferred.append(lambda m=m, q=q: mm(SB1.v(0, n), ONES.v(), q, m == 0, m == 15))
            flush()
            post_norm_residual(l, V_FFN_POST, n)

        def load_x(t0, n):
            S.phase = "load_x"
            NT = n // 128
            for t in range(NT):
                S.dma("sp", f"xl{t}", stg.v(t * D, (t + 1) * D).ap, x_d[t0 + t * 128: t0 + (t + 1) * 128, :],
                      writes=[stg.v(t * D, (t + 1) * D)])
            for c in range(16):
                pb = nb()
                for t in range(NT):
                    tr(pb.v(t * 128, (t + 1) * 128), stg.v(t * D + c * 128, t * D + (c + 1) * 128))
                cp("act" if c % 2 else "dve", chunk(X, c, n), pb.v(0, n))

        def store_x(t0, n):
            S.phase = "store_x"
            NT = n // 128
            for t in range(NT):
                for q in range(4):
                    pb = nb()
                    for cc in range(4):
                        c = q * 4 + cc
                        tr(pb.v(cc * 128, (cc + 1) * 128), X.v(c * NMAX + t * 128, c * NMAX + (t + 1) * 128))
                    cp("act" if q % 2 else "dve", stg.v(t * D + q * 512, t * D + (q + 1) * 512), pb.v(0, 512))
                S.dma("sp", f"ys{t}", y_d[t0 + t * 128: t0 + (t + 1) * 128, :], stg.v(t * D, (t + 1) * D).ap,
                      reads=[stg.v(t * D, (t + 1) * D)])

        t0 = 0
        for gi, n in enumerate(groups):
            load_x(t0, n)
            for l in range(L):
                if stage >= 1:
                    mixer(l, n, gi)
                if stage >= 2:
                    ffn(l, n)
            store_x(t0, n)
            t0 += n
        for eng in ("sp", "act", "dve", "pool"):
            S.wait_all(eng)
        S._emit_waits("pe", {s: v for s, v in S.cnt.items() if v > 0 and s != "pe"})
        build.pe_labels = S.pe_labels
        build.stats = dict(ninst=S.ninst, nwaits=S.nwaits, sbuf=total, cnt=dict(S.cnt))
    return nc


def _fm(v):
    return np.swapaxes(v.reshape(v.shape[:-1] + (16, 128)), -1, -2)


def _prep_params(inp, layers):
    L = len(layers)
    f = lambda a: np.asarray(a, dtype=np.float32)
    vecs = []
    for l in layers:
        vs = [f(inp["norm_mix_pre"])[l], f(inp["norm_mix_post"])[l], f(inp["norm_ffn_pre"])[l], f(inp["norm_ffn_post"])[l],
              f(inp["conv_b"])[l], f(inp["conv_ln_g"])[l], f(inp["conv_ln_b"])[l], f(inp["b_gate"])[l, 0], f(inp["b_gate"])[l, 1]]
        vecs.append(np.stack([_fm(v) for v in vs], axis=0))
    pv = np.stack(vecs, 0)
    pv = np.ascontiguousarray(pv.transpose(2, 0, 1, 3).reshape(128, L * NV * 16))
    cw = f(inp["conv_w"])[layers]
    cw = cw.reshape(L, CW, 16, 128).transpose(0, 3, 2, 1)
    cw = np.ascontiguousarray(cw.reshape(L, 128, 16 * CW))
    g = f(inp["sgu_ln_g"])[layers]
    b = f(inp["sgu_ln_b"])[layers]
    gb = np.concatenate([g, b], axis=-1)[:, None, :]
    gb = np.ascontiguousarray(np.broadcast_to(gb, (L, 128, 2 * D)))
    ws = f(inp["w_spatial"])[layers]
    wst = np.ascontiguousarray(ws.transpose(3, 0, 1, 2).reshape(128, L * 8 * 128))
    bsp = np.ascontiguousarray(f(inp["b_spatial"])[layers].reshape(1, L * 8 * 128))
    out = dict(pv=pv, cw=cw, gb=gb, wst=wst, bsp=bsp)
    for k_in, k_out in (("w_in", "w_in"), ("w_a_out", "w_a"), ("w_b_out", "w_b"), ("w_o", "w_o"), ("w_gate_up", "w_gu"), ("w_down", "w_d")):
        w = f(inp[k_in])
        out[k_out] = w if (L == w.shape[0] and list(layers) == list(range(L))) else np.ascontiguousarray(w[layers])
    return out


def _core_x(x, core):
    b, half = divmod(core, 2)
    xc = np.zeros((TOK, D), dtype=np.float32)
    if half == 0:
        xc[HALO:] = x[b, :SEQ // 2]
    else:
        xc[:] = x[b, SEQ // 2 - HALO:]
    hm = np.full((128, 1), float(half), dtype=np.float32)
    return xc, hm


_NC_CACHE = {}


def _get_nc(L):
    if L not in _NC_CACHE:
        _NC_CACHE[L] = build(L)
    return _NC_CACHE[L]


FUSED = True


def kernel(**inputs):
    x = np.asarray(inputs["x"], dtype=np.float32)
    cores = list(range(NCORES))
    xs = [_core_x(x, c) for c in cores]
    if FUSED:
        nc = _get_nc(DEPTH)
        p = _prep_params(inputs, list(range(DEPTH)))
        in_maps = [dict(p, x=xc, hm=hm) for (xc, hm) in xs]
        res = run_bass_kernel_spmd(nc, in_maps, core_ids=cores)
        ys = [np.asarray(r["y"]) for r in res.results]
    else:
        nc = _get_nc(1)
        cur = [xc for (xc, _) in xs]
        for l in range(DEPTH):
            p = _prep_params(inputs, [l])
            in_maps = [dict(p, x=cur[c], hm=xs[c][1]) for c in cores]
            res = run_bass_kernel_spmd(nc, in_maps, core_ids=cores)
            cur = [np.ascontiguousarray(np.asarray(r["y"], dtype=np.float32)) for r in res.results]
        ys = cur
    out = np.empty((BATCH, SEQ, D), dtype=np.float32)
    for c in cores:
        b, half = divmod(c, 2)
        out[b, half * (SEQ // 2):(half + 1) * (SEQ // 2)] = ys[c][HALO:]
    return out
```

```python
from contextlib import ExitStack

import numpy as np
import concourse.bass as bass
import concourse.mybir as mybir
from concourse.bass_utils import run_bass_kernel_spmd

F32 = mybir.dt.float32
BF16 = mybir.dt.bfloat16
AF = mybir.ActivationFunctionType
ALU = mybir.AluOpType

D = 2048
BATCH = 4
SEQ = 4096
DEPTH = 4
FF = 5632
NJ = FF // 128
CW = 31
EPS = 1e-6
NCORES = 8
HALO = 256
BOUND = (SEQ + HALO) // 2
TOK = BOUND
GROUPS = [512, 512, 512, 512, 128]
NMAX = 512
NV = 9
(V_MIX_PRE, V_MIX_POST, V_FFN_PRE, V_FFN_POST, V_CONV_B, V_CLN_G, V_CLN_B, V_BG0, V_BG1) = range(NV)
NSLOT = 4
NPE = CW
WBLK = 16 * 256


class View:
    __slots__ = ("ap", "reg")

    def __init__(self, ap, reg):
        self.ap = ap
        self.reg = reg


class Buf:
    def __init__(self, base, ap2d, off, esize, ncols, tracked=True):
        self.base = base
        self.a = ap2d
        self.off = off
        self.es = esize
        self.ncols = ncols
        self.tracked = tracked

    def v(self, lo=0, hi=None, p0=None, p1=None):
        hi = self.ncols if hi is None else hi
        ap = self.a[:, lo:hi] if p0 is None else self.a[p0:p1, lo:hi]
        reg = (self.base, self.off + lo * self.es, self.off + hi * self.es) if self.tracked else None
        return View(ap, reg)


class Sched:
    ENG = ("pe", "act", "dve", "pool", "sp")

    def __init__(self, nc, stack):
        self.nc = nc
        self.stack = stack
        self.e = {"pe": nc.tensor, "act": nc.scalar, "dve": nc.vector, "pool": nc.gpsimd, "sp": nc.sync}
        self.semh = {}
        self.cnt = {}
        for n in ("pe", "act", "dve", "pool"):
            self.new_sem(n)
        self.waited = {n: {} for n in self.ENG}
        self.recs = {}
        self.ninst = 0
        self.nwaits = 0
        self.last_pe = None
        self.phase = "setup"
        self.pe_labels = []

    def new_sem(self, name):
        self.semh[name] = self.stack.enter_context(self.nc.semaphore(name))
        self.cnt[name] = 0
        return name

    def _collect(self, reads, writes):
        need = {}
        for r in reads:
            if r is None:
                continue
            base, lo, hi = r
            for rec in self.recs.get(base, ()):
                if rec[2] and rec[0] < hi and lo < rec[1]:
                    s, v = rec[3]
                    if need.get(s, 0) < v:
                        need[s] = v
        for w in writes:
            if w is None:
                continue
            base, lo, hi = w
            for rec in self.recs.get(base, ()):
                if rec[0] < hi and lo < rec[1]:
                    s, v = rec[3]
                    if need.get(s, 0) < v:
                        need[s] = v
        return need

    def _record(self, reads, writes, tok):
        for w in writes:
            if w is None:
                continue
            base, lo, hi = w
            lst = self.recs.setdefault(base, [])
            lst[:] = [r for r in lst if not (lo <= r[0] and r[1] <= hi)]
            lst.append([lo, hi, True, tok])
        for r in reads:
            if r is None:
                continue
            base, lo, hi = r
            lst = self.recs.setdefault(base, [])
            for rec in lst:
                if (not rec[2]) and rec[0] == lo and rec[1] == hi and rec[3][0] == tok[0]:
                    if rec[3][1] < tok[1]:
                        rec[3] = tok
                    break
            else:
                lst.append([lo, hi, False, tok])

    def _emit_waits(self, eng, need):
        w = self.waited[eng]
        for s, v in need.items():
            if eng == "pe" and s == "pe":
                continue
            if s == "pe" and v > self.cnt["pe"]:
                assert v == self.cnt["pe"] + 1 and self.last_pe is not None and not self.last_pe[1]
                self.last_pe[0].then_inc(self.semh["pe"], 1)
                self.last_pe[1] = True
                self.cnt["pe"] += 1
            if w.get(s, 0) < v:
                self.e[eng].wait_ge(self.semh[s], v)
                w[s] = v
                self.nwaits += 1

    def op(self, eng, fn, reads=(), writes=(), inc=True):
        reads = [r.reg for r in reads]
        writes = [w.reg for w in writes]
        self._emit_waits(eng, self._collect(reads, writes))
        ins = fn(self.e[eng])
        self.ninst += 1
        if eng == "pe":
            self.last_pe = [ins, bool(inc)]
            self.pe_labels.append(self.phase)
        if inc:
            self.cnt[eng] += 1
            ins.then_inc(self.semh[eng], 1)
            tok = (eng, self.cnt[eng])
        else:
            tok = (eng, self.cnt[eng] + 1)
        self._record(reads, writes, tok)
        return tok

    def dma(self, eng, sem, out, in_, reads=(), writes=(), **kw):
        reads = [r if (r is None or isinstance(r, tuple)) else r.reg for r in reads]
        writes = [w if (w is None or isinstance(w, tuple)) else w.reg for w in writes]
        self._emit_waits(eng, self._collect(reads, writes))
        ins = self.e[eng].dma_start(out=out, in_=in_, **kw)
        self.ninst += 1
        self.cnt[sem] += 16
        ins.then_inc(self.semh[sem], 16)
        tok = (sem, self.cnt[sem])
        self._record(reads, writes, tok)
        return tok

    def wait_all(self, eng):
        self._emit_waits(eng, {s: v for s, v in self.cnt.items() if v > 0})


def build(L, groups=GROUPS, stage=9):
    nc = bass.Bass("TRN2", target_bir_lowering=False)
    ntok = sum(groups)
    dr = {}

    def din(name, shape):
        dr[name] = nc.dram_tensor(name, shape, F32, kind="ExternalInput").ap()
        return dr[name]

    x_d = din("x", [ntok, D])
    hm_d = din("hm", [128, 1])
    pv_d = din("pv", [128, L * NV * 16])
    cw_d = din("cw", [L, 128, 16 * CW])
    gb_d = din("gb", [L, 128, 2 * D])
    wst_d = din("wst", [128, L * 8 * 128])
    bsp_d = din("bsp", [1, L * 8 * 128])
    w_in = din("w_in", [L, D, 6 * D])
    w_a = din("w_a", [L, D, D])
    w_b = din("w_b", [L, D, D])
    w_o = din("w_o", [L, D, D])
    w_gu = din("w_gu", [L, D, 2 * FF])
    w_d = din("w_d", [L, FF, D])
    y_d = nc.dram_tensor("y", [ntok, D], F32, kind="ExternalOutput").ap()

    MATS = {"win": (w_in, 48), "wa": (w_a, 8), "wb": (w_b, 8), "wo": (w_o, 8), "wgu": (w_gu, 44), "wd": (w_d, 24)}
    sc = {}
    for l in range(L):
        for mname, (_, nb_) in MATS.items():
            sc[(mname, l)] = nc.dram_tensor(f"sc_{mname}_{l}", [nb_, 128, WBLK], BF16, kind="Internal").ap()

    with ExitStack() as st:
        S = Sched(nc, st)

        plan = []
        off = [0]
        bufs = {}

        def alloc(name, dt, es, ncols, tracked=True):
            o = off[0]
            plan.append((name, dt, es, ncols, tracked, o))
            off[0] = o + ((es * ncols + 63) // 64) * 64
            return o

        alloc("x", F32, 4, 16 * NMAX)
        o_hA = alloc("h", BF16, 2, 16 * NMAX)
        alloc("A", BF16, 2, 16 * NMAX)
        o_uvu = alloc("UB", BF16, 2, 16 * NMAX)
        alloc("VM", BF16, 2, 16 * NMAX)
        alloc("U", BF16, 2, 16 * NMAX)
        for i in range(NSLOT):
            alloc(f"W{i}", BF16, 2, WBLK)
        alloc("GB", F32, 4, 2 * D)
        for i in range(6):
            alloc(f"T{i}", F32, 4, NMAX)
        for i in range(3):
            alloc(f"hat{i}", BF16, 2, NMAX + 32)
        for i in range(12):
            alloc(f"dg{i}", BF16, 2, 128)
        alloc("identb", BF16, 2, 128)
        for i in range(4):
            alloc(f"sq{i}", BF16, 2, NMAX)
        alloc("pv", F32, 4, L * NV * 16, False)
        alloc("cw", F32, 4, 16 * CW)
        alloc("state", BF16, 2, L * 16 * 30)
        alloc("wsT", BF16, 2, L * 8 * 128)
        alloc("bspT", BF16, 2, L * 8 * 128)
        alloc("ones", BF16, 2, 128)
        alloc("ident", F32, 4, 128)
        alloc("hm", F32, 4, 1, False)
        alloc("vs1", F32, 4, 32)
        alloc("vs2", F32, 4, 32)
        alloc("vst", F32, 4, 32)
        total = off[0]
        arena = st.enter_context(nc.sbuf_tensor("arena", [128, total // 2], BF16))
        a_bf = arena[:, :]
        a_f32 = a_bf.bitcast(F32)
        for (name, dt, es, ncols, tracked, o) in plan:
            base_ap = a_f32 if dt == F32 else a_bf
            bufs[name] = Buf("arena", base_ap[:, o // es: o // es + ncols], o, es, ncols, tracked)
        bufs["Y"] = Buf("arena", a_f32[:, o_hA // 4: o_hA // 4 + 16 * NMAX], o_hA, 4, 16 * NMAX)
        bufs["C"] = Buf("arena", a_bf[:, o_uvu // 2: o_uvu // 2 + 16 * NMAX], o_uvu, 2, 16 * NMAX)
        bufs["STG"] = Buf("arena", a_f32[:, o_uvu // 4: o_uvu // 4 + 24 * NMAX], o_uvu, 4, 24 * NMAX)
        bufs["F"] = Buf("arena", a_bf[:, o_uvu // 2: o_uvu // 2 + NJ * NMAX], o_uvu, 2, NJ * NMAX)
        B = bufs
        X, H, A_, UB, VM, U, GB, Y, C, Fh = B["x"], B["h"], B["A"], B["UB"], B["VM"], B["U"], B["GB"], B["Y"], B["C"], B["F"]
        T = [B[f"T{i}"] for i in range(6)]
        HAT = [B[f"hat{i}"] for i in range(3)]
        SQ = [B[f"sq{i}"] for i in range(4)]
        PV, CWB, STATE, WST, BSPT, ONES, IDENT, HM = B["pv"], B["cw"], B["state"], B["wsT"], B["bspT"], B["ones"], B["ident"], B["hm"]
        VS1, VS2, VST = B["vs1"], B["vs2"], B["vst"]
        DG = [B[f"dg{i}"] for i in range(12)]
        IDENTB = B["identb"]
        rot_dg = [0]

        banks = []
        for i in range(8):
            h_ = st.enter_context(nc.psum_tensor(f"pb{i}", [128, 512], F32))
            banks.append(Buf(f"pb{i}", h_[:, :], 0, 4, 512))
        rot = {"b": 0, "t": 0, "q": 0, "h": 0}

        def nb():
            b = banks[rot["b"] % 4]
            rot["b"] += 1
            return b

        def nsq():
            b = SQ[rot["q"] % 4]
            rot["q"] += 1
            return b

        SB1, SB2 = banks[6], banks[7]
        CB = [banks[4], banks[5]]

        for s_ in ("cst", "gbs", "cws") + tuple(f"xl{i}" for i in range(4)) + tuple(f"ys{i}" for i in range(4)):
            S.new_sem(s_)
        for i in range(NSLOT):
            S.new_sem(f"ws{i}")
        for i in range(NSLOT):
            S.new_sem(f"wb{i}")
            S.new_sem(f"wp{i}")

        def act(out, in_, func, bias=None, scale=None, accum=None, extra=()):
            kw = {}
            if bias is not None:
                kw["bias"] = bias
            if scale is not None:
                kw["scale"] = scale
            if accum is not None:
                kw["accum_out"] = accum.ap
            w = [out] + ([accum] if accum is not None else [])
            return S.op("act", lambda e: e.activation(out.ap, in_.ap, func, **kw), reads=[in_] + list(extra), writes=w)

        def tt(eng, out, a, b, op):
            return S.op(eng, lambda e: e.tensor_tensor(out.ap, a.ap, b.ap, op), reads=[a, b], writes=[out])

        def stt(out, in0, scalar, in1, op0, op1, extra=()):
            return S.op("dve", lambda e: e.scalar_tensor_tensor(out.ap, in0.ap, scalar, in1.ap, op0, op1),
                        reads=[in0, in1] + list(extra), writes=[out])

        def ts(eng, out, in0, s1, s2, op0, op1=None, extra=()):
            if op1 is None:
                return S.op(eng, lambda e: e.tensor_scalar(out.ap, in0.ap, s1, None, op0), reads=[in0] + list(extra), writes=[out])
            return S.op(eng, lambda e: e.tensor_scalar(out.ap, in0.ap, s1, s2, op0, op1), reads=[in0] + list(extra), writes=[out])

        def rsqrt_(v):
            S.op("dve", lambda e: e.reciprocal(v.ap, v.ap), reads=[v], writes=[v])
            act(v, v, AF.Sqrt)

        def cp(eng, out, in_):
            if eng == "act":
                return S.op("act", lambda e: e.copy(out.ap, in_.ap), reads=[in_], writes=[out])
            return S.op(eng, lambda e: e.tensor_copy(out.ap, in_.ap), reads=[in_], writes=[out])

        def mm(out, lhsT, rhs, start, stop, inc=None):
            return S.op("pe", lambda e: e.matmul(out.ap, lhsT.ap, rhs.ap, start=start, stop=stop),
                        reads=[lhsT, rhs], writes=[out], inc=(stop if inc is None else inc))

        def tr(out, in_):
            return S.op("pe", lambda e: e.transpose(out.ap, in_.ap, IDENT.v().ap), reads=[in_], writes=[out])

        deferred = []

        def flush():
            while deferred:
                deferred.pop(0)()

        def pvs(l, vec, c):
            col = (l * NV + vec) * 16 + c
            return PV.v(col, col + 1).ap

        def convert_layer(l):
            for mname, (wsrc, nblk) in MATS.items():
                sem = f"cv_{mname}_{l}"
                for b in range(nblk):
                    if mname == "wd":
                        cb, kr = divmod(b, 3)
                        kk = 16 if kr < 2 else 12
                        src = wsrc[l, kr * 2048: kr * 2048 + kk * 128, cb * 256:(cb + 1) * 256]
                    else:
                        kk = 16
                        src = wsrc[l, :, b * 256:(b + 1) * 256]
                    src = src.rearrange("(k p) c -> p k c", p=128)
                    dst = sc[(mname, l)][b][:, 0:kk * 256].rearrange("p (k c) -> p k c", k=kk)
                    S.dma("pool", sem, dst, src, writes=[(f"sc_{mname}_{l}", 0, 1)])

        def layer_blocks(l):
            seq = []
            for j in range(8):
                seq += [("win", l, 8 + j, 16), ("win", l, j, 16), ("win", l, 24 + j, 16)]
            for j in range(8):
                seq.append(("win", l, 16 + j, 16))
            for j in range(8):
                seq += [("win", l, 32 + j, 16), ("win", l, 40 + j, 16), ("wa", l, j, 16), ("wb", l, j, 16)]
            for j in range(8):
                seq.append(("wo", l, j, 16))
            for j in range(22):
                seq += [("wgu", l, j, 16), ("wgu", l, 22 + j, 16)]
            for cb in range(8):
                for kr in range(3):
                    seq.append(("wd", l, cb * 3 + kr, 16 if kr < 2 else 12))
            return seq

        wseq = []
        for _g in groups:
            for l in range(L):
                lb = layer_blocks(l)
                if stage == 1:
                    lb = [b_ for b_ in lb if b_[0] not in ("wgu", "wd")]
                if stage >= 1:
                    wseq += lb
        wst_ = {"issued": 0, "next": 0}
        WS = [B[f"W{i}"] for i in range(NSLOT)]

        first_pass = len(layer_blocks(0)) * L

        def w_src(mname, l, b, kk):
            wsrc = MATS[mname][0]
            if mname == "wd":
                cb, kr = divmod(b, 3)
                src = wsrc[l, kr * 2048: kr * 2048 + kk * 128, cb * 256:(cb + 1) * 256]
            else:
                src = wsrc[l, :, b * 256:(b + 1) * 256]
            return src.rearrange("(k p) c -> p k c", p=128)

        def w_issue():
            i = wst_["issued"]
            if i >= len(wseq):
                return
            mname, l, b, kk = wseq[i]
            sl = i % NSLOT
            slot = WS[sl].v(0, kk * 256)
            blk = (f"sc_{mname}_{l}", b, b + 1)
            if i < first_pass:
                S.dma("pool", f"wp{sl}", slot.ap.rearrange("p (k c) -> p k c", k=kk), w_src(mname, l, b, kk), writes=[slot])
                S.dma("sp", f"wb{sl}", sc[(mname, l)][b][:, 0:kk * 256], slot.ap, reads=[slot], writes=[blk])
            else:
                S.dma("sp", f"ws{sl}", slot.ap, sc[(mname, l)][b][:, 0:kk * 256], reads=[blk], writes=[slot])
            wst_["issued"] = i + 1

        def wget(desc):
            i = wst_["next"]
            assert wseq[i][:3] == desc, (wseq[i], desc)
            while wst_["issued"] < min(i + NSLOT, len(wseq)):
                w_issue()
            wst_["next"] = i + 1
            return WS[i % NSLOT]

        stg = B["STG"]
        LW = L * 1024
        o_b0, o_b1 = LW, 2 * LW
        S.dma("sp", "cst", PV.v().ap, pv_d, writes=[])
        S.dma("sp", "cst", HM.v().ap, hm_d, writes=[])
        S.dma("sp", "cst", stg.v(0, LW).ap, wst_d, writes=[stg.v(0, LW)])
        S.dma("sp", "cst", stg.v(o_b0, o_b0 + LW, 0, 1).ap, bsp_d, writes=[stg.v(o_b0, o_b0 + LW)])
        cst_all = {"cst": S.cnt["cst"]}
        S._emit_waits("pool", cst_all)
        S._emit_waits("dve", cst_all)
        S.op("pool", lambda e: e.memset(ONES.v().ap, 1.0), writes=[ONES.v()])
        S.op("pool", lambda e: e.memset(IDENT.v().ap, 1.0), writes=[IDENT.v()])
        S.op("pool", lambda e: e.memset(STATE.v().ap, 0.0), writes=[STATE.v()])
        S.op("pool", lambda e: e.affine_select(out=IDENT.v().ap, in_=IDENT.v().ap, pattern=[[-1, 128]],
                                               compare_op=ALU.is_equal, fill=0.0, base=0, channel_multiplier=1),
             reads=[IDENT.v()], writes=[IDENT.v()])
        S.op("pool", lambda e: e.tensor_copy(IDENTB.v().ap, IDENT.v().ap), reads=[IDENT.v()], writes=[IDENTB.v()])
        S.op("pool", lambda e: e.affine_select(out=stg.v(0, LW).ap, in_=stg.v(0, LW).ap,
                                               pattern=[[0, L * 8], [1, 128]], compare_op=ALU.is_ge, fill=0.0,
                                               base=0, channel_multiplier=-1),
             reads=[stg.v(0, LW)], writes=[stg.v(0, LW)])
        S.op("pool", lambda e: e.tensor_copy(WST.v().ap, stg.v(0, LW).ap), reads=[stg.v(0, LW)], writes=[WST.v()])
        bs0 = stg.v(o_b0, o_b0 + LW)
        bs1 = stg.v(o_b1, o_b1 + LW)
        r0 = lambda bf, o: bf.v(o, o + LW, 0, 1).ap
        lo_bf = Fh.v(2 * LW, 3 * LW)
        S.op("dve", lambda e: e.tensor_copy(r0(BSPT, 0), r0(stg, o_b0)), reads=[bs0], writes=[BSPT.v()])
        S.op("dve", lambda e: e.tensor_copy(r0(stg, o_b1), r0(BSPT, 0)), reads=[BSPT.v()], writes=[bs1])
        S.op("dve", lambda e: e.tensor_tensor(r0(stg, o_b1), r0(stg, o_b0), r0(stg, o_b1), ALU.subtract),
             reads=[bs0, bs1], writes=[bs1])
        S.op("dve", lambda e: e.tensor_copy(r0(Fh, 2 * LW), r0(stg, o_b1)), reads=[bs1, bs0], writes=[lo_bf])
        S.dma("sp", "cst", BSPT.v(0, LW, 1, 2).ap, r0(Fh, 2 * LW), reads=[lo_bf], writes=[BSPT.v()])
        setup_need = {s_: v for s_, v in S.cnt.items() if v > 0 and not s_.startswith("cv_")}
        for eng in ("pe", "act", "dve", "pool"):
            S._emit_waits(eng, dict(setup_need))

        def chunk(buf, c, n, width=NMAX):
            return buf.v(c * width, c * width + n)

        def load_layer_consts(l):
            S.dma("sp", "gbs", GB.v().ap, gb_d[l], writes=[GB.v()])
            S.dma("sp", "cws", CWB.v().ap, cw_d[l], writes=[CWB.v()])

        def rms_to_h(l, vec, n):
            S.phase = "rms_pre"
            for c in range(16):
                q = nsq().v(0, n)
                act(q, chunk(X, c, n), AF.Square)
                mm(SB1.v(0, n), ONES.v(), q, c == 0, c == 15)
            r = T[4].v(0, n)
            ts("dve", r, SB1.v(0, n), 1.0 / D, EPS, ALU.mult, ALU.add)
            rsqrt_(r)
            for c in range(16):
                stt(chunk(H, c, n), chunk(X, c, n), pvs(l, vec, c), r, ALU.mult, ALU.mult)

        def post_norm_residual(l, vec, n):
            r = T[4].v(0, n)
            ts("dve", r, SB1.v(0, n), 1.0 / D, EPS, ALU.mult, ALU.add)
            rsqrt_(r)
            for m in range(16):
                ym = chunk(Y, m, n)
                stt(ym, ym, pvs(l, vec, m), r, ALU.mult, ALU.mult)
                tt("dve", chunk(X, m, n), chunk(X, m, n), ym, ALU.add)

        def proj_fm(wslot, mm_, src, n, nk=16, k0=0, pb=None, start=True, stop=True):
            if pb is None:
                pb = nb()
            for k in range(nk):
                mm(pb.v(0, n), wslot.v(k * 256 + mm_ * 128, k * 256 + (mm_ + 1) * 128), chunk(src, k0 + k, n),
                   start and k == 0, stop and k == nk - 1)
            return pb

        def mixer(l, n, gi):
            NT = n // 128
            load_layer_consts(l)
            rms_to_h(l, V_MIX_PRE, n)
            for j in range(8):
                S.phase = "a_gate"
                wg = wget(("win", l, 8 + j))
                sg = []
                for mm_ in range(2):
                    pb = proj_fm(wg, mm_, H, n)
                    t_ = T[(2 * j + mm_) % 4].v(0, n)
                    act(t_, pb.v(0, n), AF.Sigmoid)
                    sg.append(t_)
                S.phase = "a_in"
                wa = wget(("win", l, j))
                pbs = [proj_fm(wa, mm_, H, n) for mm_ in range(2)]
                S.phase = "a_fin_stats"
                flush()
                hats = []
                for mm_ in range(2):
                    m = 2 * j + mm_
                    hat = HAT[rot["h"] % 3]
                    rot["h"] += 1
                    sto = (l * 16 + m) * 30
                    cp("dve", hat.v(0, 30), STATE.v(sto, sto + 30))
                    tt("dve", hat.v(30, 30 + n), pbs[mm_].v(0, n), sg[mm_], ALU.mult)
                    cp("dve", STATE.v(sto, sto + 30), hat.v(n, n + 30))
                    hats.append(hat)
                S.phase = "conv"
                accs = []
                for mm_ in range(2):
                    m = 2 * j + mm_
                    acc = CB[m % 2].v(0, n)
                    accs.append(acc)
                    cwc = m * CW
                    for k in range(NPE):
                        dg = DG[rot_dg[0] % 12].v()
                        rot_dg[0] += 1
                        S.op("dve", lambda e: e.tensor_scalar(dg.ap, IDENTB.v().ap, CWB.v(cwc + k, cwc + k + 1).ap, None, ALU.mult),
                             reads=[CWB.v(cwc + k, cwc + k + 1)], writes=[dg])
                        mm(acc, dg, hats[mm_].v(k, k + n), k == 0, k == NPE - 1)
                for mm_ in range(2):
                    m = 2 * j + mm_
                    acc = accs[mm_]
                    cwc = m * CW
                    for k in range(NPE, CW):
                        stt(acc, hats[mm_].v(k, k + n), CWB.v(cwc + k, cwc + k + 1).ap, acc, ALU.mult, ALU.add)

                    def fin(m=m, acc=acc):
                        cm = chunk(C, m, n)
                        act(cm, acc, AF.Identity, bias=pvs(l, V_CONV_B, m))
                        q2 = nsq().v(0, n)
                        act(q2, acc, AF.Square, bias=pvs(l, V_CONV_B, m))
                        mm(SB1.v(0, n), ONES.v(), cm, m == 0, m == 15)
                        mm(SB2.v(0, n), ONES.v(), q2, m == 0, m == 15)
                    deferred.append(fin)
                S.phase = "v_proj"
                wv = wget(("win", l, 24 + j))
                for t in range(NT):
                    pb = nb()
                    for k in range(16):
                        mm(pb.v(0, 256), H.v(k * NMAX + t * 128, k * NMAX + (t + 1) * 128), wv.v(k * 256, (k + 1) * 256),
                           k == 0, k == 15)
                    vb = VM.v(t * D + j * 256, t * D + (j + 1) * 256)
                    act(vb, pb.v(0, 256), AF.Gelu, accum=VS1.v(t * 8 + j, t * 8 + j + 1))
                    act(nsq().v(0, 256), vb, AF.Square, accum=VS2.v(t * 8 + j, t * 8 + j + 1))
            S.phase = "a_fin_last"
            flush()
            mean = T[4].v(0, n)
            rstd = T[5].v(0, n)
            tmp = T[0].v(0, n)
            ts("dve", mean, SB1.v(0, n), 1.0 / D, None, ALU.mult)
            tt("dve", tmp, mean, mean, ALU.mult)
            stt(rstd, SB2.v(0, n), 1.0 / D, tmp, ALU.mult, ALU.subtract)
            ts("dve", rstd, rstd, EPS, None, ALU.add)
            rsqrt_(rstd)
            stt(mean, mean, -1.0, rstd, ALU.mult, ALU.mult)
            s1, s2 = VST.v(12, 12 + NT), VST.v(16, 16 + NT)
            mean_t, msq_t, rstd_t = VST.v(0, NT), VST.v(4, 4 + NT), VST.v(8, 8 + NT)
            S.op("dve", lambda e: e.tensor_reduce(s1.ap, VS1.v(0, NT * 8).ap.rearrange("p (t j) -> p t j", j=8),
                                                  mybir.AxisListType.X, ALU.add), reads=[VS1.v(0, NT * 8)], writes=[s1])
            S.op("dve", lambda e: e.tensor_reduce(s2.ap, VS2.v(0, NT * 8).ap.rearrange("p (t j) -> p t j", j=8),
                                                  mybir.AxisListType.X, ALU.add), reads=[VS2.v(0, NT * 8)], writes=[s2])
            ts("dve", mean_t, s1, 1.0 / D, None, ALU.mult)
            tt("dve", msq_t, mean_t, mean_t, ALU.mult)
            stt(rstd_t, s2, 1.0 / D, msq_t, ALU.mult, ALU.subtract)
            ts("dve", rstd_t, rstd_t, EPS, None, ALU.add)
            rsqrt_(rstd_t)
            nmr_t = VST.v(20, 20 + NT)
            stt(nmr_t, mean_t, -1.0, rstd_t, ALU.mult, ALU.mult)
            tasks = []
            rot_t = [0]

            def v_task(t, q):
                vb = VM.v(t * D + q * 512, t * D + (q + 1) * 512)
                tmp = T[rot_t[0] % 4].v(0, 512)
                rot_t[0] += 1
                act(tmp, vb, AF.Identity, bias=VST.v(20 + t, 21 + t).ap, scale=VST.v(8 + t, 9 + t).ap,
                    extra=[nmr_t, rstd_t])
                tt("dve", tmp, tmp, GB.v(q * 512, (q + 1) * 512), ALU.mult)
                tt("dve", vb, tmp, GB.v(D + q * 512, D + (q + 1) * 512), ALU.add)

            def a_task(m):
                t_ = T[rot_t[0] % 4].v(0, n)
                rot_t[0] += 1
                tt("dve", t_, chunk(C, m, n), rstd, ALU.mult)
                tt("dve", t_, t_, mean, ALU.add)
                act(chunk(A_, m, n), t_, AF.Silu, bias=pvs(l, V_CLN_B, m), scale=pvs(l, V_CLN_G, m))

            for t in range(NT):
                for q in range(4):
                    tasks.append(lambda t=t, q=q: v_task(t, q))
            for m in range(16):
                tasks.append(lambda m=m: a_task(m))
            S.phase = "u_proj"
            for j in range(8):
                wu = wget(("win", l, 16 + j))
                for mm_ in range(2):
                    pb = proj_fm(wu, mm_, H, n)
                    act(chunk(U, 2 * j + mm_, n), pb.v(0, n), AF.Gelu)
                    for _ in range(2):
                        if tasks:
                            tasks.pop(0)()
            while tasks:
                tasks.pop(0)()
            S.phase = "spatial"
            for m in range(16):
                g8 = l * 8 + m // 2
                pb = nb()
                for t in range(NT):
                    o = pb.v(t * 128, (t + 1) * 128)
                    mm(o, VM.v(t * D + m * 128, t * D + (m + 1) * 128), WST.v(g8 * 128, (g8 + 1) * 128), True, False, inc=False)
                    mm(o, ONES.v(0, 128, 0, 2), BSPT.v(g8 * 128, (g8 + 1) * 128, 0, 2), False, True, inc=(t == NT - 1))
                tt("dve", chunk(UB, m, n), pb.v(0, n), chunk(U, m, n), ALU.mult)
            S.phase = "merge"
            for j in range(8):
                g0 = []
                w0 = wget(("win", l, 32 + j))
                for mm_ in range(2):
                    pb = proj_fm(w0, mm_, H, n)
                    t_ = T[mm_].v(0, n)
                    act(t_, pb.v(0, n), AF.Sigmoid, bias=pvs(l, V_BG0, 2 * j + mm_))
                    g0.append(t_)
                g1 = []
                w1 = wget(("win", l, 40 + j))
                for mm_ in range(2):
                    pb = proj_fm(w1, mm_, H, n)
                    t_ = T[2 + mm_].v(0, n)
                    act(t_, pb.v(0, n), AF.Sigmoid, bias=pvs(l, V_BG1, 2 * j + mm_))
                    g1.append(t_)
                wa_ = wget(("wa", l, j))
                for mm_ in range(2):
                    pb = proj_fm(wa_, mm_, A_, n)
                    tt("dve", g0[mm_], pb.v(0, n), g0[mm_], ALU.mult)
                wb_ = wget(("wb", l, j))
                for mm_ in range(2):
                    pb = proj_fm(wb_, mm_, UB, n)
                    tt("dve", g1[mm_], pb.v(0, n), g1[mm_], ALU.mult)
                    tt("dve", chunk(VM, 2 * j + mm_, n), g0[mm_], g1[mm_], ALU.add)
            S.phase = "wo"
            for j in range(8):
                wo_ = wget(("wo", l, j))
                for mm_ in range(2):
                    m = 2 * j + mm_
                    pb = proj_fm(wo_, mm_, VM, n)
                    flush()
                    cp("act", chunk(Y, m, n), pb.v(0, n))
                    q = nsq().v(0, n)
                    act(q, pb.v(0, n), AF.Square)
                    deferred.append(lambda m=m, q=q: mm(SB1.v(0, n), ONES.v(), q, m == 0, m == 15))
            flush()
            post_norm_residual(l, V_MIX_POST, n)

        def ffn(l, n):
            rms_to_h(l, V_FFN_PRE, n)
            S.phase = "gate_up"
            for j in range(22):
                wg = wget(("wgu", l, j))
                sl = []
                for mm_ in range(2):
                    pb = proj_fm(wg, mm_, H, n)
                    t_ = T[(2 * j + mm_) % 4].v(0, n)
                    act(t_, pb.v(0, n), AF.Silu)
                    sl.append(t_)
                wu = wget(("wgu", l, 22 + j))
                for mm_ in range(2):
                    pb = proj_fm(wu, mm_, H, n)
                    tt("dve", chunk(Fh, 2 * j + mm_, n), pb.v(0, n), sl[mm_], ALU.mult)
            S.phase = "down"
            for cb in range(8):
                pbs = [nb(), nb()]
                for kr in range(3):
                    kk = 16 if kr < 2 else 12
                    wd_ = wget(("wd", l, cb * 3 + kr))
                    for mm_ in range(2):
                        proj_fm(wd_, mm_, Fh, n, nk=kk, k0=kr * 16, pb=pbs[mm_], start=(kr == 0), stop=(kr == 2))
                flush()
                for mm_ in range(2):
                    m = 2 * cb + mm_
                    cp("act", chunk(Y, m, n), pbs[mm_].v(0, n))
                    q = nsq().v(0, n)
                    act(q, pbs[mm_].v(0, n), AF.Square)
                    deferred.append(lambda m=m, q=q: mm(SB1.v(0, n), ONES.v(), q, m == 0, m == 15))
            flush()
            post_norm_residual(l, V_FFN_POST, n)

        def load_x(t0, n):
            S.phase = "load_x"
            NT = n // 128
            for t in range(NT):
                S.dma("sp", f"xl{t}", stg.v(t * D, (t + 1) * D).ap, x_d[t0 + t * 128: t0 + (t + 1) * 128, :],
                      writes=[stg.v(t * D, (t + 1) * D)])
            for c in range(16):
                pb = nb()
                for t in range(NT):
                    tr(pb.v(t * 128, (t + 1) * 128), stg.v(t * D + c * 128, t * D + (c + 1) * 128))
                cp("act" if c % 2 else "dve", chunk(X, c, n), pb.v(0, n))

        def store_x(t0, n):
            S.phase = "store_x"
            NT = n // 128
            for t in range(NT):
                for q in range(4):
                    pb = nb()
                    for cc in range(4):
                        c = q * 4 + cc
                        tr(pb.v(cc * 128, (cc + 1) * 128), X.v(c * NMAX + t * 128, c * NMAX + (t + 1) * 128))
                    cp("act" if q % 2 else "dve", stg.v(t * D + q * 512, t * D + (q + 1) * 512), pb.v(0, 512))
                S.dma("sp", f"ys{t}", y_d[t0 + t * 128: t0 + (t + 1) * 128, :], stg.v(t * D, (t + 1) * D).ap,
                      reads=[stg.v(t * D, (t + 1) * D)])

        t0 = 0
        for gi, n in enumerate(groups):
            load_x(t0, n)
            for l in range(L):
                if stage >= 1:
                    mixer(l, n, gi)
                if stage >= 2:
                    ffn(l, n)
            store_x(t0, n)
            t0 += n
        for eng in ("sp", "act", "dve", "pool"):
            S.wait_all(eng)
        S._emit_waits("pe", {s: v for s, v in S.cnt.items() if v > 0 and s != "pe"})
        build.pe_labels = S.pe_labels
        build.stats = dict(ninst=S.ninst, nwaits=S.nwaits, sbuf=total, cnt=dict(S.cnt))
    return nc


def _fm(v):
    return np.swapaxes(v.reshape(v.shape[:-1] + (16, 128)), -1, -2)


def _prep_params(inp, layers):
    L = len(layers)
    f = lambda a: np.asarray(a, dtype=np.float32)
    vecs = []
    for l in layers:
        vs = [f(inp["norm_mix_pre"])[l], f(inp["norm_mix_post"])[l], f(inp["norm_ffn_pre"])[l], f(inp["norm_ffn_post"])[l],
              f(inp["conv_b"])[l], f(inp["conv_ln_g"])[l], f(inp["conv_ln_b"])[l], f(inp["b_gate"])[l, 0], f(inp["b_gate"])[l, 1]]
        vecs.append(np.stack([_fm(v) for v in vs], axis=0))
    pv = np.stack(vecs, 0)
    pv = np.ascontiguousarray(pv.transpose(2, 0, 1, 3).reshape(128, L * NV * 16))
    cw = f(inp["conv_w"])[layers]
    cw = cw.reshape(L, CW, 16, 128).transpose(0, 3, 2, 1)
    cw = np.ascontiguousarray(cw.reshape(L, 128, 16 * CW))
    g = f(inp["sgu_ln_g"])[layers]
    b = f(inp["sgu_ln_b"])[layers]
    gb = np.concatenate([g, b], axis=-1)[:, None, :]
    gb = np.ascontiguousarray(np.broadcast_to(gb, (L, 128, 2 * D)))
    ws = f(inp["w_spatial"])[layers]
    wst = np.ascontiguousarray(ws.transpose(3, 0, 1, 2).reshape(128, L * 8 * 128))
    bsp = np.ascontiguousarray(f(inp["b_spatial"])[layers].reshape(1, L * 8 * 128))
    out = dict(pv=pv, cw=cw, gb=gb, wst=wst, bsp=bsp)
    for k_in, k_out in (("w_in", "w_in"), ("w_a_out", "w_a"), ("w_b_out", "w_b"), ("w_o", "w_o"), ("w_gate_up", "w_gu"), ("w_down", "w_d")):
        w = f(inp[k_in])
        out[k_out] = w if (L == w.shape[0] and list(layers) == list(range(L))) else np.ascontiguousarray(w[layers])
    return out


def _core_x(x, core):
    b, half = divmod(core, 2)
    if half == 0:
        xc = np.ascontiguousarray(x[b, :BOUND])
    else:
        xc = np.ascontiguousarray(x[b, BOUND - HALO:])
    hm = np.full((128, 1), float(half), dtype=np.float32)
    return xc, hm


_NC_CACHE = {}


def _get_nc(L):
    if L not in _NC_CACHE:
        _NC_CACHE[L] = build(L)
    return _NC_CACHE[L]


FUSED = True


def kernel(**inputs):
    x = np.asarray(inputs["x"], dtype=np.float32)
    cores = list(range(NCORES))
    xs = [_core_x(x, c) for c in cores]
    if FUSED:
        nc = _get_nc(DEPTH)
        p = _prep_params(inputs, list(range(DEPTH)))
        in_maps = [dict(p, x=xc, hm=hm) for (xc, hm) in xs]
        res = run_bass_kernel_spmd(nc, in_maps, core_ids=cores)
        ys = [np.asarray(r["y"]) for r in res.results]
    else:
        nc = _get_nc(1)
        cur = [xc for (xc, _) in xs]
        for l in range(DEPTH):
            p = _prep_params(inputs, [l])
            in_maps = [dict(p, x=cur[c], hm=xs[c][1]) for c in cores]
            res = run_bass_kernel_spmd(nc, in_maps, core_ids=cores)
            cur = [np.ascontiguousarray(np.asarray(r["y"], dtype=np.float32)) for r in res.results]
        ys = cur
    out = np.empty((BATCH, SEQ, D), dtype=np.float32)
    for c in cores:
        b, half = divmod(c, 2)
        if half == 0:
            out[b, :BOUND] = ys[c]
        else:
            out[b, BOUND:] = ys[c][HALO:]
    return out
```

```python
from contextlib import ExitStack

import numpy as np
import concourse.bass as bass
import concourse.mybir as mybir
from concourse.bass_utils import run_bass_kernel_spmd

F32 = mybir.dt.float32
BF16 = mybir.dt.bfloat16
AF = mybir.ActivationFunctionType
ALU = mybir.AluOpType

D = 2048
BATCH = 4
SEQ = 4096
DEPTH = 4
FF = 5632
NJ = FF // 128
CW = 31
EPS = 1e-6
NCORES = 8
HALO = 256
BOUND = (SEQ + HALO) // 2
TOK = BOUND
GROUPS = [512, 512, 512, 512, 128]
NMAX = 512
NV = 9
(V_MIX_PRE, V_MIX_POST, V_FFN_PRE, V_FFN_POST, V_CONV_B, V_CLN_G, V_CLN_B, V_BG0, V_BG1) = range(NV)
NSLOT = 3
NPE = 20
WBLK = 16 * 256


class View:
    __slots__ = ("ap", "reg")

    def __init__(self, ap, reg):
        self.ap = ap
        self.reg = reg


class Buf:
    def __init__(self, base, ap2d, off, esize, ncols, tracked=True):
        self.base = base
        self.a = ap2d
        self.off = off
        self.es = esize
        self.ncols = ncols
        self.tracked = tracked

    def v(self, lo=0, hi=None, p0=None, p1=None):
        hi = self.ncols if hi is None else hi
        ap = self.a[:, lo:hi] if p0 is None else self.a[p0:p1, lo:hi]
        reg = (self.base, self.off + lo * self.es, self.off + hi * self.es) if self.tracked else None
        return View(ap, reg)


class Sched:
    ENG = ("pe", "act", "dve", "pool", "sp")

    def __init__(self, nc, stack):
        self.nc = nc
        self.stack = stack
        self.e = {"pe": nc.tensor, "act": nc.scalar, "dve": nc.vector, "pool": nc.gpsimd, "sp": nc.sync}
        self.semh = {}
        self.cnt = {}
        for n in ("pe", "act", "dve", "pool"):
            self.new_sem(n)
        self.waited = {n: {} for n in self.ENG}
        self.recs = {}
        self.ninst = 0
        self.nwaits = 0
        self.last_pe = None
        self.phase = "setup"
        self.pe_labels = []

    def new_sem(self, name):
        self.semh[name] = self.stack.enter_context(self.nc.semaphore(name))
        self.cnt[name] = 0
        return name

    def _collect(self, reads, writes):
        need = {}
        for r in reads:
            if r is None:
                continue
            base, lo, hi = r
            for rec in self.recs.get(base, ()):
                if rec[2] and rec[0] < hi and lo < rec[1]:
                    s, v = rec[3]
                    if need.get(s, 0) < v:
                        need[s] = v
        for w in writes:
            if w is None:
                continue
            base, lo, hi = w
            for rec in self.recs.get(base, ()):
                if rec[0] < hi and lo < rec[1]:
                    s, v = rec[3]
                    if need.get(s, 0) < v:
                        need[s] = v
        return need

    def _record(self, reads, writes, tok):
        for w in writes:
            if w is None:
                continue
            base, lo, hi = w
            lst = self.recs.setdefault(base, [])
            lst[:] = [r for r in lst if not (lo <= r[0] and r[1] <= hi)]
            lst.append([lo, hi, True, tok])
        for r in reads:
            if r is None:
                continue
            base, lo, hi = r
            lst = self.recs.setdefault(base, [])
            for rec in lst:
                if (not rec[2]) and rec[0] == lo and rec[1] == hi and rec[3][0] == tok[0]:
                    if rec[3][1] < tok[1]:
                        rec[3] = tok
                    break
            else:
                lst.append([lo, hi, False, tok])

    def _emit_waits(self, eng, need):
        w = self.waited[eng]
        for s, v in need.items():
            if eng == "pe" and s == "pe":
                continue
            if s == "pe" and v > self.cnt["pe"]:
                assert v == self.cnt["pe"] + 1 and self.last_pe is not None and not self.last_pe[1]
                self.last_pe[0].then_inc(self.semh["pe"], 1)
                self.last_pe[1] = True
                self.cnt["pe"] += 1
            if w.get(s, 0) < v:
                self.e[eng].wait_ge(self.semh[s], v)
                w[s] = v
                self.nwaits += 1

    def op(self, eng, fn, reads=(), writes=(), inc=True):
        reads = [r.reg for r in reads]
        writes = [w.reg for w in writes]
        self._emit_waits(eng, self._collect(reads, writes))
        ins = fn(self.e[eng])
        self.ninst += 1
        if eng == "pe":
            self.last_pe = [ins, bool(inc)]
            self.pe_labels.append(self.phase)
        if inc:
            self.cnt[eng] += 1
            ins.then_inc(self.semh[eng], 1)
            tok = (eng, self.cnt[eng])
        else:
            tok = (eng, self.cnt[eng] + 1)
        self._record(reads, writes, tok)
        return tok

    def dma(self, eng, sem, out, in_, reads=(), writes=(), **kw):
        reads = [r if (r is None or isinstance(r, tuple)) else r.reg for r in reads]
        writes = [w if (w is None or isinstance(w, tuple)) else w.reg for w in writes]
        self._emit_waits(eng, self._collect(reads, writes))
        ins = self.e[eng].dma_start(out=out, in_=in_, **kw)
        self.ninst += 1
        self.cnt[sem] += 16
        ins.then_inc(self.semh[sem], 16)
        tok = (sem, self.cnt[sem])
        self._record(reads, writes, tok)
        return tok

    def wait_all(self, eng):
        self._emit_waits(eng, {s: v for s, v in self.cnt.items() if v > 0})


def build(L, groups=GROUPS, stage=9):
    nc = bass.Bass("TRN2", target_bir_lowering=False)
    ntok = sum(groups)
    dr = {}

    def din(name, shape):
        dr[name] = nc.dram_tensor(name, shape, F32, kind="ExternalInput").ap()
        return dr[name]

    x_d = din("x", [ntok, D])
    hm_d = din("hm", [128, 1])
    pv_d = din("pv", [128, L * NV * 16])
    cw_d = din("cw", [L, 128, 16 * CW])
    gb_d = din("gb", [L, 128, 2 * D])
    wst_d = din("wst", [128, L * 8 * 128])
    bsp_d = din("bsp", [1, L * 8 * 128])
    w_in = din("w_in", [L, D, 6 * D])
    w_a = din("w_a", [L, D, D])
    w_b = din("w_b", [L, D, D])
    w_o = din("w_o", [L, D, D])
    w_gu = din("w_gu", [L, D, 2 * FF])
    w_d = din("w_d", [L, FF, D])
    y_d = nc.dram_tensor("y", [ntok, D], F32, kind="ExternalOutput").ap()

    MATS = {"win": (w_in, 48), "wa": (w_a, 8), "wb": (w_b, 8), "wo": (w_o, 8), "wgu": (w_gu, 44), "wd": (w_d, 24)}
    sc = {}
    for l in range(L):
        for mname, (_, nb_) in MATS.items():
            sc[(mname, l)] = nc.dram_tensor(f"sc_{mname}_{l}", [nb_, 128, WBLK], BF16, kind="Internal").ap()

    with ExitStack() as st:
        S = Sched(nc, st)

        plan = []
        off = [0]
        bufs = {}

        def alloc(name, dt, es, ncols, tracked=True):
            o = off[0]
            plan.append((name, dt, es, ncols, tracked, o))
            off[0] = o + ((es * ncols + 63) // 64) * 64
            return o

        alloc("x", F32, 4, 16 * NMAX)
        o_hA = alloc("h", BF16, 2, 16 * NMAX)
        alloc("A", BF16, 2, 16 * NMAX)
        o_uvu = alloc("UB", BF16, 2, 16 * NMAX)
        alloc("VM", BF16, 2, 16 * NMAX)
        alloc("U", BF16, 2, 16 * NMAX)
        for i in range(NSLOT):
            alloc(f"W{i}", BF16, 2, WBLK)
        alloc("GB", F32, 4, 2 * D)
        for i in range(6):
            alloc(f"T{i}", F32, 4, NMAX)
        for i in range(2):
            alloc(f"cacc{i}", F32, 4, NMAX)
        for i in range(3):
            alloc(f"hat{i}", BF16, 2, NMAX + 32)
        for i in range(12):
            alloc(f"dg{i}", BF16, 2, 128)
        alloc("identb", BF16, 2, 128)
        for i in range(4):
            alloc(f"sq{i}", BF16, 2, NMAX)
        alloc("pv", F32, 4, L * NV * 16, False)
        alloc("cw", F32, 4, 16 * CW)
        alloc("state", BF16, 2, L * 16 * 30)
        alloc("wsT", BF16, 2, L * 8 * 128)
        alloc("bspT", BF16, 2, L * 8 * 128)
        alloc("ones", BF16, 2, 128)
        alloc("ident", F32, 4, 128)
        alloc("hm", F32, 4, 1, False)
        alloc("vs1", F32, 4, 32)
        alloc("vs2", F32, 4, 32)
        alloc("vst", F32, 4, 32)
        total = off[0]
        arena = st.enter_context(nc.sbuf_tensor("arena", [128, total // 2], BF16))
        a_bf = arena[:, :]
        a_f32 = a_bf.bitcast(F32)
        for (name, dt, es, ncols, tracked, o) in plan:
            base_ap = a_f32 if dt == F32 else a_bf
            bufs[name] = Buf("arena", base_ap[:, o // es: o // es + ncols], o, es, ncols, tracked)
        bufs["Y"] = Buf("arena", a_f32[:, o_hA // 4: o_hA // 4 + 16 * NMAX], o_hA, 4, 16 * NMAX)
        bufs["C"] = Buf("arena", a_bf[:, o_uvu // 2: o_uvu // 2 + 16 * NMAX], o_uvu, 2, 16 * NMAX)
        bufs["STG"] = Buf("arena", a_f32[:, o_uvu // 4: o_uvu // 4 + 24 * NMAX], o_uvu, 4, 24 * NMAX)
        bufs["F"] = Buf("arena", a_bf[:, o_uvu // 2: o_uvu // 2 + NJ * NMAX], o_uvu, 2, NJ * NMAX)
        B = bufs
        X, H, A_, UB, VM, U, GB, Y, C, Fh = B["x"], B["h"], B["A"], B["UB"], B["VM"], B["U"], B["GB"], B["Y"], B["C"], B["F"]
        T = [B[f"T{i}"] for i in range(6)]
        CACC = [B[f"cacc{i}"] for i in range(2)]
        HAT = [B[f"hat{i}"] for i in range(3)]
        SQ = [B[f"sq{i}"] for i in range(4)]
        PV, CWB, STATE, WST, BSPT, ONES, IDENT, HM = B["pv"], B["cw"], B["state"], B["wsT"], B["bspT"], B["ones"], B["ident"], B["hm"]
        VS1, VS2, VST = B["vs1"], B["vs2"], B["vst"]
        DG = [B[f"dg{i}"] for i in range(12)]
        IDENTB = B["identb"]
        rot_dg = [0]

        banks = []
        for i in range(8):
            h_ = st.enter_context(nc.psum_tensor(f"pb{i}", [128, 512], F32))
            banks.append(Buf(f"pb{i}", h_[:, :], 0, 4, 512))
        rot = {"b": 0, "t": 0, "q": 0, "h": 0}

        def nb():
            b = banks[rot["b"] % 4]
            rot["b"] += 1
            return b

        def nsq():
            b = SQ[rot["q"] % 4]
            rot["q"] += 1
            return b

        SB1, SB2 = banks[6], banks[7]
        CB = [banks[4], banks[5]]

        for s_ in ("cst", "gbs", "cws") + tuple(f"xl{i}" for i in range(4)) + tuple(f"ys{i}" for i in range(4)):
            S.new_sem(s_)
        for i in range(NSLOT):
            S.new_sem(f"ws{i}")
        for i in range(NSLOT):
            S.new_sem(f"wb{i}")
            S.new_sem(f"wp{i}")

        def act(out, in_, func, bias=None, scale=None, accum=None, extra=()):
            kw = {}
            if bias is not None:
                kw["bias"] = bias
            if scale is not None:
                kw["scale"] = scale
            if accum is not None:
                kw["accum_out"] = accum.ap
            w = [out] + ([accum] if accum is not None else [])
            return S.op("act", lambda e: e.activation(out.ap, in_.ap, func, **kw), reads=[in_] + list(extra), writes=w)

        def tt(eng, out, a, b, op):
            return S.op(eng, lambda e: e.tensor_tensor(out.ap, a.ap, b.ap, op), reads=[a, b], writes=[out])

        def stt(out, in0, scalar, in1, op0, op1, extra=()):
            return S.op("dve", lambda e: e.scalar_tensor_tensor(out.ap, in0.ap, scalar, in1.ap, op0, op1),
                        reads=[in0, in1] + list(extra), writes=[out])

        def ts(eng, out, in0, s1, s2, op0, op1=None, extra=()):
            if op1 is None:
                return S.op(eng, lambda e: e.tensor_scalar(out.ap, in0.ap, s1, None, op0), reads=[in0] + list(extra), writes=[out])
            return S.op(eng, lambda e: e.tensor_scalar(out.ap, in0.ap, s1, s2, op0, op1), reads=[in0] + list(extra), writes=[out])

        def rsqrt_(v):
            S.op("dve", lambda e: e.reciprocal(v.ap, v.ap), reads=[v], writes=[v])
            act(v, v, AF.Sqrt)

        def cp(eng, out, in_):
            if eng == "act":
                return S.op("act", lambda e: e.copy(out.ap, in_.ap), reads=[in_], writes=[out])
            return S.op(eng, lambda e: e.tensor_copy(out.ap, in_.ap), reads=[in_], writes=[out])

        def mm(out, lhsT, rhs, start, stop, inc=None):
            return S.op("pe", lambda e: e.matmul(out.ap, lhsT.ap, rhs.ap, start=start, stop=stop),
                        reads=[lhsT, rhs], writes=[out], inc=(stop if inc is None else inc))

        def tr(out, in_):
            return S.op("pe", lambda e: e.transpose(out.ap, in_.ap, IDENT.v().ap), reads=[in_], writes=[out])

        deferred = []

        def flush():
            while deferred:
                deferred.pop(0)()

        def pvs(l, vec, c):
            col = (l * NV + vec) * 16 + c
            return PV.v(col, col + 1).ap

        def convert_layer(l):
            for mname, (wsrc, nblk) in MATS.items():
                sem = f"cv_{mname}_{l}"
                for b in range(nblk):
                    if mname == "wd":
                        cb, kr = divmod(b, 3)
                        kk = 16 if kr < 2 else 12
                        src = wsrc[l, kr * 2048: kr * 2048 + kk * 128, cb * 256:(cb + 1) * 256]
                    else:
                        kk = 16
                        src = wsrc[l, :, b * 256:(b + 1) * 256]
                    src = src.rearrange("(k p) c -> p k c", p=128)
                    dst = sc[(mname, l)][b][:, 0:kk * 256].rearrange("p (k c) -> p k c", k=kk)
                    S.dma("pool", sem, dst, src, writes=[(f"sc_{mname}_{l}", 0, 1)])

        def layer_blocks(l):
            seq = []
            for j in range(8):
                seq += [("win", l, 8 + j, 16), ("win", l, j, 16), ("win", l, 24 + j, 16)]
            for j in range(8):
                seq.append(("win", l, 16 + j, 16))
            for j in range(8):
                seq += [("win", l, 32 + j, 16), ("win", l, 40 + j, 16), ("wa", l, j, 16), ("wb", l, j, 16)]
            for j in range(8):
                seq.append(("wo", l, j, 16))
            for j in range(22):
                seq += [("wgu", l, j, 16), ("wgu", l, 22 + j, 16)]
            for cb in range(8):
                for kr in range(3):
                    seq.append(("wd", l, cb * 3 + kr, 16 if kr < 2 else 12))
            return seq

        wseq = []
        for _g in groups:
            for l in range(L):
                lb = layer_blocks(l)
                if stage == 1:
                    lb = [b_ for b_ in lb if b_[0] not in ("wgu", "wd")]
                if stage >= 1:
                    wseq += lb
        wst_ = {"issued": 0, "next": 0}
        WS = [B[f"W{i}"] for i in range(NSLOT)]

        first_pass = len(layer_blocks(0)) * L

        def w_src(mname, l, b, kk):
            wsrc = MATS[mname][0]
            if mname == "wd":
                cb, kr = divmod(b, 3)
                src = wsrc[l, kr * 2048: kr * 2048 + kk * 128, cb * 256:(cb + 1) * 256]
            else:
                src = wsrc[l, :, b * 256:(b + 1) * 256]
            return src.rearrange("(k p) c -> p k c", p=128)

        def w_issue():
            i = wst_["issued"]
            if i >= len(wseq):
                return
            mname, l, b, kk = wseq[i]
            sl = i % NSLOT
            slot = WS[sl].v(0, kk * 256)
            blk = (f"sc_{mname}_{l}", b, b + 1)
            if i < first_pass:
                S.dma("pool", f"wp{sl}", slot.ap.rearrange("p (k c) -> p k c", k=kk), w_src(mname, l, b, kk), writes=[slot])
                S.dma("sp", f"wb{sl}", sc[(mname, l)][b][:, 0:kk * 256], slot.ap, reads=[slot], writes=[blk])
            else:
                S.dma("sp", f"ws{sl}", slot.ap, sc[(mname, l)][b][:, 0:kk * 256], reads=[blk], writes=[slot])
            wst_["issued"] = i + 1

        def wget(desc):
            i = wst_["next"]
            assert wseq[i][:3] == desc, (wseq[i], desc)
            while wst_["issued"] < min(i + NSLOT, len(wseq)):
                w_issue()
            wst_["next"] = i + 1
            return WS[i % NSLOT]

        stg = B["STG"]
        LW = L * 1024
        o_b0, o_b1 = LW, 2 * LW
        S.dma("sp", "cst", PV.v().ap, pv_d, writes=[])
        S.dma("sp", "cst", HM.v().ap, hm_d, writes=[])
        S.dma("sp", "cst", stg.v(0, LW).ap, wst_d, writes=[stg.v(0, LW)])
        S.dma("sp", "cst", stg.v(o_b0, o_b0 + LW, 0, 1).ap, bsp_d, writes=[stg.v(o_b0, o_b0 + LW)])
        cst_all = {"cst": S.cnt["cst"]}
        S._emit_waits("pool", cst_all)
        S._emit_waits("dve", cst_all)
        S.op("pool", lambda e: e.memset(ONES.v().ap, 1.0), writes=[ONES.v()])
        S.op("pool", lambda e: e.memset(IDENT.v().ap, 1.0), writes=[IDENT.v()])
        S.op("pool", lambda e: e.memset(STATE.v().ap, 0.0), writes=[STATE.v()])
        S.op("pool", lambda e: e.affine_select(out=IDENT.v().ap, in_=IDENT.v().ap, pattern=[[-1, 128]],
                                               compare_op=ALU.is_equal, fill=0.0, base=0, channel_multiplier=1),
             reads=[IDENT.v()], writes=[IDENT.v()])
        S.op("pool", lambda e: e.tensor_copy(IDENTB.v().ap, IDENT.v().ap), reads=[IDENT.v()], writes=[IDENTB.v()])
        S.op("pool", lambda e: e.affine_select(out=stg.v(0, LW).ap, in_=stg.v(0, LW).ap,
                                               pattern=[[0, L * 8], [1, 128]], compare_op=ALU.is_ge, fill=0.0,
                                               base=0, channel_multiplier=-1),
             reads=[stg.v(0, LW)], writes=[stg.v(0, LW)])
        S.op("pool", lambda e: e.tensor_copy(WST.v().ap, stg.v(0, LW).ap), reads=[stg.v(0, LW)], writes=[WST.v()])
        bs0 = stg.v(o_b0, o_b0 + LW)
        bs1 = stg.v(o_b1, o_b1 + LW)
        r0 = lambda bf, o: bf.v(o, o + LW, 0, 1).ap
        lo_bf = Fh.v(2 * LW, 3 * LW)
        S.op("dve", lambda e: e.tensor_copy(r0(BSPT, 0), r0(stg, o_b0)), reads=[bs0], writes=[BSPT.v()])
        S.op("dve", lambda e: e.tensor_copy(r0(stg, o_b1), r0(BSPT, 0)), reads=[BSPT.v()], writes=[bs1])
        S.op("dve", lambda e: e.tensor_tensor(r0(stg, o_b1), r0(stg, o_b0), r0(stg, o_b1), ALU.subtract),
             reads=[bs0, bs1], writes=[bs1])
        S.op("dve", lambda e: e.tensor_copy(r0(Fh, 2 * LW), r0(stg, o_b1)), reads=[bs1, bs0], writes=[lo_bf])
        S.dma("sp", "cst", BSPT.v(0, LW, 1, 2).ap, r0(Fh, 2 * LW), reads=[lo_bf], writes=[BSPT.v()])
        setup_need = {s_: v for s_, v in S.cnt.items() if v > 0 and not s_.startswith("cv_")}
        for eng in ("pe", "act", "dve", "pool"):
            S._emit_waits(eng, dict(setup_need))

        def chunk(buf, c, n, width=NMAX):
            return buf.v(c * width, c * width + n)

        def load_layer_consts(l):
            S.dma("sp", "gbs", GB.v().ap, gb_d[l], writes=[GB.v()])
            S.dma("sp", "cws", CWB.v().ap, cw_d[l], writes=[CWB.v()])

        def rms_to_h(l, vec, n):
            S.phase = "rms_pre"
            for c in range(16):
                q = nsq().v(0, n)
                act(q, chunk(X, c, n), AF.Square)
                mm(SB1.v(0, n), ONES.v(), q, c == 0, c == 15)
            r = T[4].v(0, n)
            ts("dve", r, SB1.v(0, n), 1.0 / D, EPS, ALU.mult, ALU.add)
            rsqrt_(r)
            for c in range(16):
                stt(chunk(H, c, n), chunk(X, c, n), pvs(l, vec, c), r, ALU.mult, ALU.mult)

        def post_norm_residual(l, vec, n):
            r = T[4].v(0, n)
            ts("dve", r, SB1.v(0, n), 1.0 / D, EPS, ALU.mult, ALU.add)
            rsqrt_(r)
            for m in range(16):
                ym = chunk(Y, m, n)
                stt(ym, ym, pvs(l, vec, m), r, ALU.mult, ALU.mult)
                tt("dve", chunk(X, m, n), chunk(X, m, n), ym, ALU.add)

        def proj_fm(wslot, mm_, src, n, nk=16, k0=0, pb=None, start=True, stop=True):
            if pb is None:
                pb = nb()
            for k in range(nk):
                mm(pb.v(0, n), wslot.v(k * 256 + mm_ * 128, k * 256 + (mm_ + 1) * 128), chunk(src, k0 + k, n),
                   start and k == 0, stop and k == nk - 1)
            return pb

        def mixer(l, n, gi):
            NT = n // 128
            load_layer_consts(l)
            rms_to_h(l, V_MIX_PRE, n)
            for j in range(8):
                S.phase = "a_gate"
                wg = wget(("win", l, 8 + j))
                sg = []
                for mm_ in range(2):
                    pb = proj_fm(wg, mm_, H, n)
                    t_ = T[(2 * j + mm_) % 4].v(0, n)
                    act(t_, pb.v(0, n), AF.Sigmoid)
                    sg.append(t_)
                S.phase = "a_in"
                wa = wget(("win", l, j))
                pbs = [proj_fm(wa, mm_, H, n) for mm_ in range(2)]
                S.phase = "a_fin_stats"
                flush()
                hats = []
                for mm_ in range(2):
                    m = 2 * j + mm_
                    hat = HAT[rot["h"] % 3]
                    rot["h"] += 1
                    sto = (l * 16 + m) * 30
                    cp("dve", hat.v(0, 30), STATE.v(sto, sto + 30))
                    tt("dve", hat.v(30, 30 + n), pbs[mm_].v(0, n), sg[mm_], ALU.mult)
                    cp("dve", STATE.v(sto, sto + 30), hat.v(n, n + 30))
                    hats.append(hat)
                S.phase = "conv"
                accs = []
                for mm_ in range(2):
                    m = 2 * j + mm_
                    acc = CB[m % 2].v(0, n)
                    accs.append(acc)
                    cwc = m * CW
                    for k in range(NPE):
                        dg = DG[rot_dg[0] % 12].v()
                        rot_dg[0] += 1
                        S.op("dve", lambda e: e.tensor_scalar(dg.ap, IDENTB.v().ap, CWB.v(cwc + k, cwc + k + 1).ap, None, ALU.mult),
                             reads=[CWB.v(cwc + k, cwc + k + 1)], writes=[dg])
                        mm(acc, dg, hats[mm_].v(k, k + n), k == 0, k == NPE - 1)
                for mm_ in range(2):
                    m = 2 * j + mm_
                    acc = accs[mm_]
                    cwc = m * CW
                    acc2 = CACC[mm_].v(0, n)
                    if NPE < CW:
                        ts("dve", acc2, hats[mm_].v(NPE, NPE + n), CWB.v(cwc + NPE, cwc + NPE + 1).ap, None, ALU.mult)
                        for k in range(NPE + 1, CW):
                            stt(acc2, hats[mm_].v(k, k + n), CWB.v(cwc + k, cwc + k + 1).ap, acc2, ALU.mult, ALU.add)

                    def fin(m=m, acc=acc, acc2=acc2):
                        cm = chunk(C, m, n)
                        if NPE < CW:
                            tt("dve", acc, acc, acc2, ALU.add)
                        act(cm, acc, AF.Identity, bias=pvs(l, V_CONV_B, m))
                        q2 = nsq().v(0, n)
                        act(q2, acc, AF.Square, bias=pvs(l, V_CONV_B, m))
                        mm(SB1.v(0, n), ONES.v(), cm, m == 0, m == 15)
                        mm(SB2.v(0, n), ONES.v(), q2, m == 0, m == 15)
                    deferred.append(fin)
                S.phase = "v_proj"
                wv = wget(("win", l, 24 + j))
                for t in range(NT):
                    pb = nb()
                    for k in range(16):
                        mm(pb.v(0, 256), H.v(k * NMAX + t * 128, k * NMAX + (t + 1) * 128), wv.v(k * 256, (k + 1) * 256),
                           k == 0, k == 15)
                    vb = VM.v(t * D + j * 256, t * D + (j + 1) * 256)
                    act(vb, pb.v(0, 256), AF.Gelu, accum=VS1.v(t * 8 + j, t * 8 + j + 1))
                    act(nsq().v(0, 256), vb, AF.Square, accum=VS2.v(t * 8 + j, t * 8 + j + 1))
            S.phase = "a_fin_last"
            flush()
            mean = T[4].v(0, n)
            rstd = T[5].v(0, n)
            tmp = T[0].v(0, n)
            ts("dve", mean, SB1.v(0, n), 1.0 / D, None, ALU.mult)
            tt("dve", tmp, mean, mean, ALU.mult)
            stt(rstd, SB2.v(0, n), 1.0 / D, tmp, ALU.mult, ALU.subtract)
            ts("dve", rstd, rstd, EPS, None, ALU.add)
            rsqrt_(rstd)
            stt(mean, mean, -1.0, rstd, ALU.mult, ALU.mult)
            s1, s2 = VST.v(12, 12 + NT), VST.v(16, 16 + NT)
            mean_t, msq_t, rstd_t = VST.v(0, NT), VST.v(4, 4 + NT), VST.v(8, 8 + NT)
            S.op("dve", lambda e: e.tensor_reduce(s1.ap, VS1.v(0, NT * 8).ap.rearrange("p (t j) -> p t j", j=8),
                                                  mybir.AxisListType.X, ALU.add), reads=[VS1.v(0, NT * 8)], writes=[s1])
            S.op("dve", lambda e: e.tensor_reduce(s2.ap, VS2.v(0, NT * 8).ap.rearrange("p (t j) -> p t j", j=8),
                                                  mybir.AxisListType.X, ALU.add), reads=[VS2.v(0, NT * 8)], writes=[s2])
            ts("dve", mean_t, s1, 1.0 / D, None, ALU.mult)
            tt("dve", msq_t, mean_t, mean_t, ALU.mult)
            stt(rstd_t, s2, 1.0 / D, msq_t, ALU.mult, ALU.subtract)
            ts("dve", rstd_t, rstd_t, EPS, None, ALU.add)
            rsqrt_(rstd_t)
            nmr_t = VST.v(20, 20 + NT)
            stt(nmr_t, mean_t, -1.0, rstd_t, ALU.mult, ALU.mult)
            tasks = []
            rot_t = [0]

            def v_task(t, q):
                vb = VM.v(t * D + q * 512, t * D + (q + 1) * 512)
                tmp = T[rot_t[0] % 4].v(0, 512)
                rot_t[0] += 1
                act(tmp, vb, AF.Identity, bias=VST.v(20 + t, 21 + t).ap, scale=VST.v(8 + t, 9 + t).ap,
                    extra=[nmr_t, rstd_t])
                tt("dve", tmp, tmp, GB.v(q * 512, (q + 1) * 512), ALU.mult)
                tt("dve", vb, tmp, GB.v(D + q * 512, D + (q + 1) * 512), ALU.add)

            def a_task(m):
                t_ = T[rot_t[0] % 4].v(0, n)
                rot_t[0] += 1
                tt("dve", t_, chunk(C, m, n), rstd, ALU.mult)
                tt("dve", t_, t_, mean, ALU.add)
                act(chunk(A_, m, n), t_, AF.Silu, bias=pvs(l, V_CLN_B, m), scale=pvs(l, V_CLN_G, m))

            for t in range(NT):
                for q in range(4):
                    tasks.append(lambda t=t, q=q: v_task(t, q))
            for m in range(16):
                tasks.append(lambda m=m: a_task(m))
            S.phase = "u_proj"
            for j in range(8):
                wu = wget(("win", l, 16 + j))
                for mm_ in range(2):
                    pb = proj_fm(wu, mm_, H, n)
                    act(chunk(U, 2 * j + mm_, n), pb.v(0, n), AF.Gelu)
                    for _ in range(2):
                        if tasks:
                            tasks.pop(0)()
            while tasks:
                tasks.pop(0)()
            S.phase = "spatial"
            for m in range(16):
                g8 = l * 8 + m // 2
                pb = nb()
                for t in range(NT):
                    o = pb.v(t * 128, (t + 1) * 128)
                    mm(o, VM.v(t * D + m * 128, t * D + (m + 1) * 128), WST.v(g8 * 128, (g8 + 1) * 128), True, False, inc=False)
                    mm(o, ONES.v(0, 128, 0, 2), BSPT.v(g8 * 128, (g8 + 1) * 128, 0, 2), False, True, inc=(t == NT - 1))
                tt("dve", chunk(UB, m, n), pb.v(0, n), chunk(U, m, n), ALU.mult)
            S.phase = "merge"
            for j in range(8):
                g0 = []
                w0 = wget(("win", l, 32 + j))
                for mm_ in range(2):
                    pb = proj_fm(w0, mm_, H, n)
                    t_ = T[mm_].v(0, n)
                    act(t_, pb.v(0, n), AF.Sigmoid, bias=pvs(l, V_BG0, 2 * j + mm_))
                    g0.append(t_)
                g1 = []
                w1 = wget(("win", l, 40 + j))
                for mm_ in range(2):
                    pb = proj_fm(w1, mm_, H, n)
                    t_ = T[2 + mm_].v(0, n)
                    act(t_, pb.v(0, n), AF.Sigmoid, bias=pvs(l, V_BG1, 2 * j + mm_))
                    g1.append(t_)
                wa_ = wget(("wa", l, j))
                for mm_ in range(2):
                    pb = proj_fm(wa_, mm_, A_, n)
                    tt("dve", g0[mm_], pb.v(0, n), g0[mm_], ALU.mult)
                wb_ = wget(("wb", l, j))
                for mm_ in range(2):
                    pb = proj_fm(wb_, mm_, UB, n)
                    tt("dve", g1[mm_], pb.v(0, n), g1[mm_], ALU.mult)
                    tt("dve", chunk(VM, 2 * j + mm_, n), g0[mm_], g1[mm_], ALU.add)
            S.phase = "wo"
            for j in range(8):
                wo_ = wget(("wo", l, j))
                for mm_ in range(2):
                    m = 2 * j + mm_
                    pb = proj_fm(wo_, mm_, VM, n)
                    flush()
                    cp("act", chunk(Y, m, n), pb.v(0, n))
                    q = nsq().v(0, n)
                    act(q, pb.v(0, n), AF.Square)
                    deferred.append(lambda m=m, q=q: mm(SB1.v(0, n), ONES.v(), q, m == 0, m == 15))
            flush()
            post_norm_residual(l, V_MIX_POST, n)

        def ffn(l, n):
            rms_to_h(l, V_FFN_PRE, n)
            S.phase = "gate_up"
            for j in range(22):
                wg = wget(("wgu", l, j))
                sl = []
                for mm_ in range(2):
                    pb = proj_fm(wg, mm_, H, n)
                    t_ = T[(2 * j + mm_) % 4].v(0, n)
                    act(t_, pb.v(0, n), AF.Silu)
                    sl.append(t_)
                wu = wget(("wgu", l, 22 + j))
                for mm_ in range(2):
                    pb = proj_fm(wu, mm_, H, n)
                    tt("dve", chunk(Fh, 2 * j + mm_, n), pb.v(0, n), sl[mm_], ALU.mult)
            S.phase = "down"
            for cb in range(8):
                pbs = [nb(), nb()]
                for kr in range(3):
                    kk = 16 if kr < 2 else 12
                    wd_ = wget(("wd", l, cb * 3 + kr))
                    for mm_ in range(2):
                        proj_fm(wd_, mm_, Fh, n, nk=kk, k0=kr * 16, pb=pbs[mm_], start=(kr == 0), stop=(kr == 2))
                flush()
                for mm_ in range(2):
                    m = 2 * cb + mm_
                    cp("act", chunk(Y, m, n), pbs[mm_].v(0, n))
                    q = nsq().v(0, n)
                    act(q, pbs[mm_].v(0, n), AF.Square)
                    deferred.append(lambda m=m, q=q: mm(SB1.v(0, n), ONES.v(), q, m == 0, m == 15))
            flush()
            post_norm_residual(l, V_FFN_POST, n)

        def load_x(t0, n):
            S.phase = "load_x"
            NT = n // 128
            for t in range(NT):
                S.dma("sp", f"xl{t}", stg.v(t * D, (t + 1) * D).ap, x_d[t0 + t * 128: t0 + (t + 1) * 128, :],
                      writes=[stg.v(t * D, (t + 1) * D)])
            for c in range(16):
                pb = nb()
                for t in range(NT):
                    tr(pb.v(t * 128, (t + 1) * 128), stg.v(t * D + c * 128, t * D + (c + 1) * 128))
                cp("act" if c % 2 else "dve", chunk(X, c, n), pb.v(0, n))

        def store_x(t0, n):
            S.phase = "store_x"
            NT = n // 128
            for t in range(NT):
                for q in range(4):
                    pb = nb()
                    for cc in range(4):
                        c = q * 4 + cc
                        tr(pb.v(cc * 128, (cc + 1) * 128), X.v(c * NMAX + t * 128, c * NMAX + (t + 1) * 128))
                    cp("act" if q % 2 else "dve", stg.v(t * D + q * 512, t * D + (q + 1) * 512), pb.v(0, 512))
                S.dma("sp", f"ys{t}", y_d[t0 + t * 128: t0 + (t + 1) * 128, :], stg.v(t * D, (t + 1) * D).ap,
                      reads=[stg.v(t * D, (t + 1) * D)])

        t0 = 0
        for gi, n in enumerate(groups):
            load_x(t0, n)
            for l in range(L):
                if stage >= 1:
                    mixer(l, n, gi)
                if stage >= 2:
                    ffn(l, n)
            store_x(t0, n)
            t0 += n
        for eng in ("sp", "act", "dve", "pool"):
            S.wait_all(eng)
        S._emit_waits("pe", {s: v for s, v in S.cnt.items() if v > 0 and s != "pe"})
        build.pe_labels = S.pe_labels
        build.stats = dict(ninst=S.ninst, nwaits=S.nwaits, sbuf=total, cnt=dict(S.cnt))
    return nc


def _fm(v):
    return np.swapaxes(v.reshape(v.shape[:-1] + (16, 128)), -1, -2)


def _prep_params(inp, layers):
    L = len(layers)
    f = lambda a: np.asarray(a, dtype=np.float32)
    vecs = []
    for l in layers:
        vs = [f(inp["norm_mix_pre"])[l], f(inp["norm_mix_post"])[l], f(inp["norm_ffn_pre"])[l], f(inp["norm_ffn_post"])[l],
              f(inp["conv_b"])[l], f(inp["conv_ln_g"])[l], f(inp["conv_ln_b"])[l], f(inp["b_gate"])[l, 0], f(inp["b_gate"])[l, 1]]
        vecs.append(np.stack([_fm(v) for v in vs], axis=0))
    pv = np.stack(vecs, 0)
    pv = np.ascontiguousarray(pv.transpose(2, 0, 1, 3).reshape(128, L * NV * 16))
    cw = f(inp["conv_w"])[layers]
    cw = cw.reshape(L, CW, 16, 128).transpose(0, 3, 2, 1)
    cw = np.ascontiguousarray(cw.reshape(L, 128, 16 * CW))
    g = f(inp["sgu_ln_g"])[layers]
    b = f(inp["sgu_ln_b"])[layers]
    gb = np.concatenate([g, b], axis=-1)[:, None, :]
    gb = np.ascontiguousarray(np.broadcast_to(gb, (L, 128, 2 * D)))
    ws = f(inp["w_spatial"])[layers]
    wst = np.ascontiguousarray(ws.transpose(3, 0, 1, 2).reshape(128, L * 8 * 128))
    bsp = np.ascontiguousarray(f(inp["b_spatial"])[layers].reshape(1, L * 8 * 128))
    out = dict(pv=pv, cw=cw, gb=gb, wst=wst, bsp=bsp)
    for k_in, k_out in (("w_in", "w_in"), ("w_a_out", "w_a"), ("w_b_out", "w_b"), ("w_o", "w_o"), ("w_gate_up", "w_gu"), ("w_down", "w_d")):
        w = f(inp[k_in])
        out[k_out] = w if (L == w.shape[0] and list(layers) == list(range(L))) else np.ascontiguousarray(w[layers])
    return out


def _core_x(x, core):
    b, half = divmod(core, 2)
    if half == 0:
        xc = np.ascontiguousarray(x[b, :BOUND])
    else:
        xc = np.ascontiguousarray(x[b, BOUND - HALO:])
    hm = np.full((128, 1), float(half), dtype=np.float32)
    return xc, hm


_NC_CACHE = {}


def _get_nc(L):
    if L not in _NC_CACHE:
        _NC_CACHE[L] = build(L)
    return _NC_CACHE[L]


FUSED = True


def kernel(**inputs):
    x = np.asarray(inputs["x"], dtype=np.float32)
    cores = list(range(NCORES))
    xs = [_core_x(x, c) for c in cores]
    if FUSED:
        nc = _get_nc(DEPTH)
        p = _prep_params(inputs, list(range(DEPTH)))
        in_maps = [dict(p, x=xc, hm=hm) for (xc, hm) in xs]
        res = run_bass_kernel_spmd(nc, in_maps, core_ids=cores)
        ys = [np.asarray(r["y"]) for r in res.results]
    else:
        nc = _get_nc(1)
        cur = [xc for (xc, _) in xs]
        for l in range(DEPTH):
            p = _prep_params(inputs, [l])
            in_maps = [dict(p, x=cur[c], hm=xs[c][1]) for c in cores]
            res = run_bass_kernel_spmd(nc, in_maps, core_ids=cores)
            cur = [np.ascontiguousarray(np.asarray(r["y"], dtype=np.float32)) for r in res.results]
        ys = cur
    out = np.empty((BATCH, SEQ, D), dtype=np.float32)
    for c in cores:
        b, half = divmod(c, 2)
        if half == 0:
            out[b, :BOUND] = ys[c]
        else:
            out[b, BOUND:] = ys[c][HALO:]
    return out
```
